# Optimizing a Trainium2 kernel written in Bass

```python
import jax, jax.numpy as jnp
from jax import lax
import numpy as np

D_MODEL = 2048
BATCH = 4
SEQ = 2048
DEPTH = 1
DEC_BATCH = 128
DEC_SEQ = 8
PAST_LEN = 16384
PAGE_SIZE = 128

D_FF = 5632
SSM_DINNER = D_MODEL
SSM_HEADDIM = 64
SSM_HEADS = SSM_DINNER // SSM_HEADDIM
SSM_GROUPS = 4
SSM_HPG = SSM_HEADS // SSM_GROUPS
SSM_STATE = 128
SSM_CONV = 4
SSM_CONV_DIM = SSM_DINNER + 2 * SSM_GROUPS * SSM_STATE
SSM_CHUNK = 128
CC_DIM = D_MODEL // 2
CC_KERNEL = 31
N_BRANCHES = 2
IN_COLS = SSM_DINNER + SSM_CONV_DIM + SSM_HEADS + 2 * CC_DIM + N_BRANCHES * D_MODEL
EPS = 1e-6

kernel_name = "hybrid_ssd_conformerconv_gated_macaron_step"


def rmsnorm(x, g):
    xf = x.astype(jnp.float32)
    y = xf * lax.rsqrt(jnp.mean(xf * xf, axis=-1, keepdims=True) + EPS)
    return (y * g.astype(jnp.float32)).astype(x.dtype)


def layernorm(x, g, b):
    xf = x.astype(jnp.float32)
    mu = jnp.mean(xf, axis=-1, keepdims=True)
    var = jnp.mean(jnp.square(xf - mu), axis=-1, keepdims=True)
    y = (xf - mu) * lax.rsqrt(var + EPS)
    return (y * g.astype(jnp.float32) + b.astype(jnp.float32)).astype(x.dtype)


def swiglu_ffn(x, wg, wu, wd):
    return (jax.nn.silu(x @ wg) * (x @ wu)) @ wd


def causal_depthwise_conv(x, buf, w, b):
    k = w.shape[0]
    xp = jnp.concatenate([buf.astype(x.dtype), x], axis=1)
    y = lax.conv_general_dilated(xp, w[:, None, :].astype(x.dtype), window_strides=(1,), padding='VALID',
                                 dimension_numbers=('NWC', 'WIO', 'NWC'), feature_group_count=x.shape[-1])
    new_buf = xp[:, xp.shape[1] - (k - 1):]
    return y + b.astype(x.dtype), new_buf


def ssd_scan(x, dt, A, B, C, h0):
    b, L = x.shape[0], x.shape[1]
    q = min(SSM_CHUNK, L)
    pad = (-L) % q
    if pad:
        padf = lambda t: jnp.pad(t, [(0, 0), (0, pad)] + [(0, 0)] * (t.ndim - 2))
        x, dt, B, C = padf(x), padf(dt), padf(B), padf(C)
    nc = (L + pad) // q
    G, E, P, N = SSM_GROUPS, SSM_HPG, SSM_HEADDIM, SSM_STATE
    xc = x.reshape(b, nc, q, G, E, P)
    dtc = dt.reshape(b, nc, q, G, E)
    Bc = B.reshape(b, nc, q, G, N)
    Cc = C.reshape(b, nc, q, G, N)
    cum = jnp.cumsum(dtc * A.reshape(G, E), axis=2)
    causal = jnp.tril(jnp.ones((q, q), dtype=bool))
    seg = cum[:, :, :, None] - cum[:, :, None, :]
    decay_ts = jnp.exp(jnp.where(causal[None, None, :, :, None, None], seg, -jnp.inf))
    cb = jnp.einsum('bctgn,bcsgn->bctsg', Cc, Bc)
    w_ts = cb[..., None] * decay_ts * dtc[:, :, None]
    y_diag = jnp.einsum('bctsge,bcsgep->bctgep', w_ts, xc)
    decay_end = jnp.exp(cum[:, :, -1:] - cum)
    chunk_states = jnp.einsum('bcsgn,bcsge,bcsgep->bcgepn', Bc, decay_end * dtc, xc)
    chunk_decay = jnp.exp(cum[:, :, -1])

    def step(h, inp):
        dec, st = inp
        return dec[..., None, None] * h + st, h

    h_final, h_starts = lax.scan(step, h0.reshape(b, G, E, P, N),
                                 (jnp.moveaxis(chunk_decay, 1, 0), jnp.moveaxis(chunk_states, 1, 0)))
    h_starts = jnp.moveaxis(h_starts, 0, 1)
    y_off = jnp.einsum('bctgn,bctge,bcgepn->bctgep', Cc, jnp.exp(cum), h_starts)
    y = (y_diag + y_off).reshape(b, nc * q, SSM_HEADS, P)[:, :L]
    return y, h_final.reshape(b, SSM_HEADS, P, N)


def token_mixer(h, st_ssm, st_sconv, st_cc, w_in, ssm_conv_w, ssm_conv_b, ssm_dt_bias, ssm_A_log, ssm_D,
                ssm_norm, w_ssd_out, cc_conv_w, cc_conv_b, cc_ln_g, cc_ln_b, w_cc_out, w_o):
    b, L, _ = h.shape
    proj = h @ w_in
    i0 = SSM_DINNER
    i1 = i0 + SSM_CONV_DIM
    i2 = i1 + SSM_HEADS
    i3 = i2 + 2 * CC_DIM
    z, xbc, dt_raw, glu_in, gate_logits = jnp.split(proj, [i0, i1, i2, i3], axis=-1)
    xbc, new_sconv = causal_depthwise_conv(xbc, st_sconv, ssm_conv_w, ssm_conv_b)
    xbc = jax.nn.silu(xbc)
    xs, Bm, Cm = jnp.split(xbc, [SSM_DINNER, SSM_DINNER + SSM_GROUPS * SSM_STATE], axis=-1)
    xs = xs.reshape(b, L, SSM_HEADS, SSM_HEADDIM).astype(jnp.float32)
    dt = jax.nn.softplus(dt_raw.astype(jnp.float32) + ssm_dt_bias.astype(jnp.float32))
    A = -jnp.exp(ssm_A_log.astype(jnp.float32))
    y, new_ssm = ssd_scan(xs, dt, A,
                          Bm.reshape(b, L, SSM_GROUPS, SSM_STATE).astype(jnp.float32),
                          Cm.reshape(b, L, SSM_GROUPS, SSM_STATE).astype(jnp.float32),
                          st_ssm.astype(jnp.float32))
    y = y + ssm_D.astype(jnp.float32)[:, None] * xs
    y = y.reshape(b, L, SSM_DINNER).astype(h.dtype)
    y = rmsnorm(y * jax.nn.silu(z), ssm_norm)
    ssd_out = y @ w_ssd_out
    ga, gb = jnp.split(glu_in, 2, axis=-1)
    u = ga * jax.nn.sigmoid(gb)
    u, new_cc = causal_depthwise_conv(u, st_cc, cc_conv_w, cc_conv_b)
    u = jax.nn.silu(layernorm(u, cc_ln_g, cc_ln_b))
    cc_out = u @ w_cc_out
    g_ssd, g_cc = jnp.split(jax.nn.sigmoid(gate_logits), 2, axis=-1)
    out = (g_ssd * ssd_out + g_cc * cc_out) @ w_o
    return out, new_ssm.astype(h.dtype), new_sconv, new_cc


def decoder_layer(x, st_ssm, st_sconv, st_cc, ffn_w, mix_w, norms):
    (f1g, f1u, f1d, f2g, f2u, f2d) = ffn_w
    (n_f1_pre, n_f1_post, n_mix_pre, n_mix_post, n_f2_pre, n_f2_post) = norms
    x = x + 0.5 * rmsnorm(swiglu_ffn(rmsnorm(x, n_f1_pre), f1g, f1u, f1d), n_f1_post)
    m, new_ssm, new_sconv, new_cc = token_mixer(rmsnorm(x, n_mix_pre), st_ssm, st_sconv, st_cc, *mix_w)
    x = x + rmsnorm(m, n_mix_post)
    x = x + 0.5 * rmsnorm(swiglu_ffn(rmsnorm(x, n_f2_pre), f2g, f2u, f2d), n_f2_post)
    return x, new_ssm, new_sconv, new_cc


def setup_inputs(seed: int = 0) -> dict:
    key = jax.random.key(seed)
    ks = iter(jax.random.split(key, 40))
    nrm = lambda shape, scale: jax.random.normal(next(ks), shape, jnp.float32) * scale
    gain = lambda shape: 1.0 + nrm(shape, 0.02)
    dtv = jnp.exp(jax.random.uniform(next(ks), (DEPTH, SSM_HEADS), jnp.float32, np.log(1e-3), np.log(1e-1)))
    inp = {
        "x_prompt": nrm((BATCH, SEQ, D_MODEL), 1.0),
        "x_sample": nrm((DEC_BATCH, DEC_SEQ, D_MODEL), 1.0),
        "state_ssm": nrm((DEPTH, DEC_BATCH, SSM_HEADS, SSM_HEADDIM, SSM_STATE), 0.5),
        "state_ssm_conv": nrm((DEPTH, DEC_BATCH, SSM_CONV - 1, SSM_CONV_DIM), 1.0),
        "state_cc_conv": nrm((DEPTH, DEC_BATCH, CC_KERNEL - 1, CC_DIM), 0.5),
        "ffn1_pre_norm": gain((DEPTH, D_MODEL)),
        "ffn1_post_norm": gain((DEPTH, D_MODEL)),
        "ffn1_w_gate": nrm((DEPTH, D_MODEL, D_FF), D_MODEL ** -0.5),
        "ffn1_w_up": nrm((DEPTH, D_MODEL, D_FF), D_MODEL ** -0.5),
        "ffn1_w_down": nrm((DEPTH, D_FF, D_MODEL), D_FF ** -0.5),
        "mix_pre_norm": gain((DEPTH, D_MODEL)),
        "mix_post_norm": gain((DEPTH, D_MODEL)),
        "w_in": nrm((DEPTH, D_MODEL, IN_COLS), D_MODEL ** -0.5),
        "ssm_conv_w": nrm((DEPTH, SSM_CONV, SSM_CONV_DIM), SSM_CONV ** -0.5),
        "ssm_conv_b": nrm((DEPTH, SSM_CONV_DIM), 0.02),
        "ssm_dt_bias": dtv + jnp.log(-jnp.expm1(-dtv)),
        "ssm_A_log": jnp.log(jax.random.uniform(next(ks), (DEPTH, SSM_HEADS), jnp.float32, 1.0, 16.0)),
        "ssm_D": gain((DEPTH, SSM_HEADS)),
        "ssm_norm": gain((DEPTH, SSM_DINNER)),
        "w_ssd_out": nrm((DEPTH, SSM_DINNER, D_MODEL), SSM_DINNER ** -0.5),
        "cc_conv_w": nrm((DEPTH, CC_KERNEL, CC_DIM), CC_KERNEL ** -0.5),
        "cc_conv_b": nrm((DEPTH, CC_DIM), 0.02),
        "cc_ln_g": gain((DEPTH, CC_DIM)),
        "cc_ln_b": nrm((DEPTH, CC_DIM), 0.02),
        "w_cc_out": nrm((DEPTH, CC_DIM, D_MODEL), CC_DIM ** -0.5),
        "w_o": nrm((DEPTH, D_MODEL, D_MODEL), D_MODEL ** -0.5),
        "ffn2_pre_norm": gain((DEPTH, D_MODEL)),
        "ffn2_post_norm": gain((DEPTH, D_MODEL)),
        "ffn2_w_gate": nrm((DEPTH, D_MODEL, D_FF), D_MODEL ** -0.5),
        "ffn2_w_up": nrm((DEPTH, D_MODEL, D_FF), D_MODEL ** -0.5),
        "ffn2_w_down": nrm((DEPTH, D_FF, D_MODEL), D_FF ** -0.5),
    }
    return inp


def reference(x_prompt, x_sample, state_ssm, state_ssm_conv, state_cc_conv,
              ffn1_pre_norm, ffn1_post_norm, ffn1_w_gate, ffn1_w_up, ffn1_w_down,
              mix_pre_norm, mix_post_norm, w_in, ssm_conv_w, ssm_conv_b, ssm_dt_bias, ssm_A_log, ssm_D,
              ssm_norm, w_ssd_out, cc_conv_w, cc_conv_b, cc_ln_g, cc_ln_b, w_cc_out, w_o,
              ffn2_pre_norm, ffn2_post_norm, ffn2_w_gate, ffn2_w_up, ffn2_w_down):
    dt_p = x_prompt.dtype
    xp, xs = x_prompt, x_sample
    p_ssm, p_sconv, p_cc = [], [], []
    s_ssm, s_sconv, s_cc = [], [], []
    for l in range(DEPTH):
        ffn_w = (ffn1_w_gate[l], ffn1_w_up[l], ffn1_w_down[l], ffn2_w_gate[l], ffn2_w_up[l], ffn2_w_down[l])
        norms = (ffn1_pre_norm[l], ffn1_post_norm[l], mix_pre_norm[l], mix_post_norm[l],
                 ffn2_pre_norm[l], ffn2_post_norm[l])
        mix_w = (w_in[l], ssm_conv_w[l], ssm_conv_b[l], ssm_dt_bias[l], ssm_A_log[l], ssm_D[l], ssm_norm[l],
                 w_ssd_out[l], cc_conv_w[l], cc_conv_b[l], cc_ln_g[l], cc_ln_b[l], w_cc_out[l], w_o[l])
        z_ssm = jnp.zeros((BATCH, SSM_HEADS, SSM_HEADDIM, SSM_STATE), dt_p)
        z_sconv = jnp.zeros((BATCH, SSM_CONV - 1, SSM_CONV_DIM), dt_p)
        z_cc = jnp.zeros((BATCH, CC_KERNEL - 1, CC_DIM), dt_p)
        xp, a, b_, c = decoder_layer(xp, z_ssm, z_sconv, z_cc, ffn_w, mix_w, norms)
        p_ssm.append(a); p_sconv.append(b_); p_cc.append(c)
        xs, a, b_, c = decoder_layer(xs, state_ssm[l], state_ssm_conv[l], state_cc_conv[l], ffn_w, mix_w, norms)
        s_ssm.append(a); s_sconv.append(b_); s_cc.append(c)
    return (xp, xs, jnp.stack(p_ssm), jnp.stack(p_sconv), jnp.stack(p_cc),
            jnp.stack(s_ssm), jnp.stack(s_sconv), jnp.stack(s_cc))
```

```python
import numpy as np
import concourse.bass as bass
import concourse.mybir as mybir

F32 = mybir.dt.float32
BF16 = mybir.dt.bfloat16
F32R = mybir.dt.float32r
I32 = mybir.dt.int32
ALU = mybir.AluOpType
AF = mybir.ActivationFunctionType

ENGS = ("pe", "act", "dve", "pool", "sp")


class Tile:
    __slots__ = ("name", "ap", "last_w", "readers")

    def __init__(self, name, ap):
        self.name = name
        self.ap = ap
        self.last_w = None
        self.readers = []

    def __getitem__(self, idx):
        return self.ap[idx]


class Op:
    __slots__ = ("eng", "fn", "reads", "writes", "is_dma", "deps", "signals",
                 "sig", "dsem", "dval", "name")

    def __init__(self, eng, fn, reads, writes, is_dma, name):
        self.eng = eng
        self.fn = fn
        self.reads = reads
        self.writes = writes
        self.is_dma = is_dma
        self.deps = []
        self.signals = False
        self.sig = None
        self.dsem = None
        self.dval = None
        self.name = name


class Prog:
    def __init__(self, nc, n_dma_sems=24):
        self.nc = nc
        self.ops = []
        self.n_dma_sems = n_dma_sems

    def op(self, eng, fn, reads=(), writes=(), name=""):
        o = Op(eng, fn, list(reads), list(writes), False, name)
        self._track(o)
        return o

    def dma(self, queue, out_ap, in_ap, reads=(), writes=(), name=""):
        def fn(e, out_ap=out_ap, in_ap=in_ap):
            return e.dma_start(out=out_ap, in_=in_ap)
        o = Op(queue, fn, list(reads), list(writes), True, name)
        self._track(o)
        return o

    def _track(self, o):
        deps = []
        for t in o.reads:
            if t.last_w is not None:
                deps.append(t.last_w)
        for t in o.writes:
            if t.last_w is not None:
                deps.append(t.last_w)
            deps.extend(t.readers)
        seen = set()
        for d in deps:
            if id(d) in seen or d is o:
                continue
            seen.add(id(d))
            if (not d.is_dma) and (not o.is_dma) and d.eng == o.eng:
                if o.eng == "pe":
                    continue
            o.deps.append(d)
            d.signals = True
        for t in o.reads:
            t.readers.append(o)
        for t in o.writes:
            t.last_w = o
            t.readers = []
        self.ops.append(o)

    def emit(self, final_wait_eng="sp"):
        nc = self.nc
        for o in self.ops:
            if o.is_dma:
                o.signals = True
        cnt = {e: 0 for e in ENGS}
        for o in self.ops:
            if o.is_dma:
                continue
            if o.signals:
                cnt[o.eng] += 1
                o.sig = cnt[o.eng]
        import contextlib
        with contextlib.ExitStack() as st:
            esem = {e: st.enter_context(nc.semaphore("s_" + e)) for e in ENGS}
            dsems = {q: [st.enter_context(nc.semaphore(f"d_{q}{i}")) for i in range(self.n_dma_sems)]
                     for q in ("sp", "act", "pool")}
            duse = {q: [0] * self.n_dma_sems for q in dsems}
            dnext = {q: 0 for q in dsems}
            dma_ops = []
            for o in self.ops:
                if o.is_dma:
                    q = o.eng
                    i = dnext[q]
                    dnext[q] = (i + 1) % self.n_dma_sems
                    duse[q][i] += 1
                    o.dsem = (q, i)
                    o.dval = 16 * duse[q][i]
                    dma_ops.append(o)
            block = st.enter_context(nc.Block())
            by_eng = {e: [o for o in self.ops if o.eng == e] for e in ENGS}

            def replay(e, eng):
                known = {}

                def wait(key, sem, val):
                    if known.get(key, 0) >= val:
                        return
                    known[key] = val
                    e.wait_ge(sem, val)

                for o in by_eng[eng]:
                    for d in o.deps:
                        if d.is_dma:
                            wait(("d",) + d.dsem, dsems[d.dsem[0]][d.dsem[1]], d.dval)
                        else:
                            wait(("e", d.eng), esem[d.eng], d.sig)
                    if o.is_dma and o.dval > 16:
                        q, i = o.dsem
                        wait(("d", q, i), dsems[q][i], o.dval - 16)
                    inst = o.fn(e)
                    if o.is_dma:
                        q, i = o.dsem
                        inst.then_inc(dsems[q][i], 16)
                    elif o.signals:
                        inst.then_inc(esem[eng], 1)
                if eng == final_wait_eng:
                    last = {}
                    for o in dma_ops:
                        last[o.dsem] = max(last.get(o.dsem, 0), o.dval)
                    for (q, i), v in last.items():
                        wait(("d", q, i), dsems[q][i], v)
                    for en in ENGS:
                        if cnt[en] > 0 and en != eng:
                            wait(("e", en), esem[en], cnt[en])

            block.tensor(lambda e: replay(e, "pe"))
            block.scalar(lambda e: replay(e, "act"))
            block.vector(lambda e: replay(e, "dve"))
            block.gpsimd(lambda e: replay(e, "pool"))
            block.sync(lambda e: replay(e, "sp"))

import contextlib
from concourse.bass_utils import run_bass_kernel_spmd

D = 2048
DFF = 5632
NCH = 16
EPS = 1e-6
NEG = -30000.0


def build(dbg_stop=None):
    nc = bass.Bass("TRN2", target_bir_lowering=False)
    P = Prog(nc, n_dma_sems=24)
    st = contextlib.ExitStack()

    def din(name, shape):
        return nc.dram_tensor(name, list(shape), F32, kind="ExternalInput").ap()

    def dout(name, shape):
        return nc.dram_tensor(name, list(shape), F32, kind="ExternalOutput").ap()

    xpre = din("xpre", [1024, D]); xmain = din("xmain", [1024, D]); xsam = din("xsam", [128, D])
    flag_d = din("flag", [128, 1])
    st_ssm = din("st_ssm", [16, 2048, 128]); st_sconv = din("st_sconv", [48, 3072]); st_cc = din("st_cc", [480, 1024])
    vecD = din("vecD", [7, D])
    vec3072 = din("vec3072", [5, 3072])
    vec1024 = din("vec1024", [34, 1024])
    vecH = din("vecH", [3, 32])
    f1g = din("f1g", [44, 128, 16, 128]); f1u = din("f1u", [44, 128, 16, 128]); f1d = din("f1d", [32, 128, 22, 128])
    f2g = din("f2g", [44, 128, 16, 128]); f2u = din("f2u", [44, 128, 16, 128]); f2d = din("f2d", [32, 128, 22, 128])
    win_z = din("win_z", [16, 128, 16, 128]); win_xbc = din("win_xbc", [24, 128, 16, 128]); win_dt = din("win_dt", [1, 128, 16, 32])
    win_glu = din("win_glu", [16, 128, 16, 128]); win_gate = din("win_gate", [32, 128, 16, 128])
    w_sso = din("w_sso", [16, 128, 16, 128]); w_cco = din("w_cco", [16, 128, 8, 128]); w_o = din("w_o", [16, 128, 16, 128])
    y_main = dout("y_main", [1024, D]); y_sam = dout("y_sam", [128, D])
    o_ssm_p = dout("o_ssm_p", [2048, 128]); o_sconv_p = dout("o_sconv_p", [3, 3072]); o_cc_p = dout("o_cc_p", [30, 1024])
    o_ssm_s = dout("o_ssm_s", [16, 2048, 128]); o_sconv_s = dout("o_sconv_s", [48, 3072]); o_cc_s = dout("o_cc_s", [480, 1024])

    def sbt(name, shape, dt):
        return st.enter_context(nc.sbuf_tensor(name, list(shape), dt))

    def pst(name, shape, dt):
        return st.enter_context(nc.psum_tensor(name, list(shape), dt))

    TG = 576
    with st:
        st.enter_context(nc.allow_low_precision("bf16 matmul operands, fp32 accumulation"))
        st.enter_context(nc.allow_non_contiguous_dma("small strided constant loads"))
        R_t = sbt("R", [128, NCH, TG], F32); R = [Tile(f"R{c}", R_t[:, c, :]) for c in range(NCH)]
        Y_t = sbt("Y", [128, NCH, TG], F32); Y = [Tile(f"Y{c}", Y_t[:, c, :]) for c in range(NCH)]
        xn_t = sbt("xn", [128, NCH, TG], BF16); xn = [Tile(f"xn{c}", xn_t[:, c, :]) for c in range(NCH)]
        h_t = sbt("h", [128, 24, TG], BF16); H = [Tile(f"h{c}", h_t[:, c, :]) for c in range(24)]
        mg_t = sbt("mg", [128, NCH * TG], BF16)
        MG = [Tile(f"mg{c}", mg_t[:, c * TG:(c + 1) * TG]) for c in range(NCH)]
        xtok = Tile("xtok", mg_t[:, 0:2048]); xw = Tile("xw", mg_t[:, 2048:4096])
        btok = Tile("btok", mg_t[:, 4096:4608]); btm = Tile("btm", mg_t[:, 4608:5120])
        sTb = Tile("sTb", mg_t[:, 5120:7168]); hTb = Tile("hTb", mg_t[:, 7168:9216])
        WG = Tile("WG", mg_t[:, 5120:6144]); SCG = Tile("SCG", mg_t[:, 6144:7168])
        ALIAS = [xtok, xw, btok, btm, sTb, hTb, WG, SCG]
        NWB = 3
        wring = [Tile(f"w{i}", sbt(f"w{i}", [128, 22, 128], BF16)) for i in range(NWB)]
        wcount = [0]
        tmpA = [Tile(f"tA{i}", sbt(f"tA{i}", [128, TG], F32)) for i in range(2)]
        sqr = [Tile(f"sq{i}", sbt(f"sq{i}", [128, TG], BF16)) for i in range(2)]
        rstd = Tile("rstd", sbt("rstd", [128, TG], F32))
        aux = Tile("aux", sbt("aux", [128, TG], F32))
        dtF = aux
        xin = [Tile(f"xin{i}", sbt(f"xin{i}", [128, D], F32)) for i in range(2)]
        otok = xin
        cin = [Tile(f"cin{i}", sbt(f"cin{i}", [128, 2, 128], F32)) for i in range(2)]
        cin_n = [0]
        identf = Tile("identf", sbt("identf", [128, 128], F32))
        identb = Tile("identb", sbt("identb", [128, 128], BF16))
        ones_b = Tile("ones_b", sbt("ones_b", [128, 128], BF16))
        ones_f = Tile("ones_f", sbt("ones_f", [128, 128], F32))
        gD = Tile("gD", sbt("gD", [128, NCH, 7], F32))
        g3072 = Tile("g3072", sbt("g3072", [128, 24, 5], F32))
        g1024 = Tile("g1024", sbt("g1024", [128, 8, 34], F32))
        gH = Tile("gH", sbt("gH", [32, 4], F32))
        A_b = Tile("A_b", sbt("A_b", [128, 32], F32))
        Dcol = Tile("Dcol", sbt("Dcol", [128, 16], F32))
        flag = Tile("flag", sbt("flagt", [128, 1], F32))
        dummy = Tile("dummy", sbt("fdummy", [128, 2], F32))
        M1p = Tile("M1p", sbt("M1p", [128, 128], F32)); NEGp = Tile("NEGp", sbt("NEGp", [128, 128], F32))
        M1s = Tile("M1s", sbt("M1s", [128, 128], F32)); NEGs = Tile("NEGs", sbt("NEGs", [128, 128], F32))
        M2s = Tile("M2s", sbt("M2s", [128, 128], F32))
        seqind = Tile("seqind", sbt("seqind", [128, 8], F32))
        hist_x = Tile("hist_x", sbt("hist_x", [128, 24, 3], F32))
        hist_u = Tile("hist_u", sbt("hist_u", [128, 8, 30], F32))
        stg = [Tile(f"stg{i}", sbt(f"stg{i}", [128, 32 + 512], F32)) for i in range(2)]
        sstg = [Tile(f"sstg{i}", sbt(f"sstg{i}", [128, 8, 38], F32)) for i in range(2)]
        cacc = tmpA
        hT32 = Tile("hT32", sbt("hT32", [128, 2048], F32))
        cbt = Tile("cbt", sbt("cbt", [128, 4, 128], F32))
        dt_tm = Tile("dt_tm", sbt("dt_tm", [128, 32], F32)); dtA = Tile("dtA", sbt("dtA", [128, 32], F32))
        ncum = Tile("ncum", sbt("ncum", [128, 32], F32)); dedt = Tile("dedt", sbt("dedt", [128, 32], F32))
        decb = Tile("decb", sbt("decb", [128, 32], F32)); D_b = decb
        dnat = Tile("dnat", sbt("dnat", [128, 16], F32))
        wS = Tile("wS", sbt("wS", [128, 8, 64], BF16))
        ACC = [Tile(f"acc{i}", pst(f"acc{i}", [128, 2, 512], F32)) for i in range(3)]
        acc_i = [0]
        pm_t = [pst(f"pm{i}", [128, 4, 128], F32) for i in range(2)]
        PM = [Tile(f"pm{i}", pm_t[i]) for i in range(2)]
        pm_i = [0]

        reserved = [None]

        def next_acc():
            a = ACC[acc_i[0] % 3]; acc_i[0] += 1
            if a is reserved[0]:
                a = ACC[acc_i[0] % 3]; acc_i[0] += 1
            return a

        def next_pm():
            a = PM[pm_i[0] % 2]; pm_i[0] += 1; return a

        def V(ap, hw):
            return ap.rearrange("p (a b) -> p a b", b=hw)

        def halves(T):
            return (1, 512) if T == 512 else (2, T // 2)

        def wreq(Wt, idx, ncols, KC):
            t = wring[wcount[0] % NWB]; wcount[0] += 1
            P.dma("pool", t[:, 0:KC, 0:ncols], Wt[idx], writes=[t])
            return t

        def mm(wt, KC, ncols, rhs_tiles, T, extra_reads=()):
            return mm_multi([(wt, KC)], ncols, rhs_tiles, T)

        def mm_multi(wts, ncols, rhs_tiles, T):
            nh, hw = halves(T)
            acc = next_acc()
            KT = sum(k for _, k in wts)

            def fn(e):
                last = None
                kk = 0
                for wt, KC in wts:
                    for k in range(KC):
                        for a in range(nh):
                            last = e.matmul(acc[0:ncols, a, 0:hw], lhsT=wt[:, k, 0:ncols],
                                            rhs=rhs_tiles[kk][:, a * hw:(a + 1) * hw], start=(kk == 0), stop=(kk == KT - 1))
                        kk += 1
                return last
            P.op("pe", fn, reads=[w for w, _ in wts] + list(rhs_tiles[:KT]), writes=[acc])
            return acc

        def act(fn, reads, writes):
            return P.op("act", fn, reads=reads, writes=writes)

        def dve(fn, reads, writes):
            return P.op("dve", fn, reads=reads, writes=writes)

        def fence(tiles):
            dve(lambda e: e.memset(dummy[:, 0:1], 0.0), [], [dummy] + list(tiles))

        def sumsq_norm(S, T, gi, outs):
            nh, hw = halves(T)
            acc = next_acc()
            for c in range(NCH):
                sq = sqr[c % 2]
                act(lambda e, c=c, sq=sq: e.activation(out=sq[:, 0:T], in_=S[c][:, 0:T], func=AF.Square), [S[c]], [sq])

                def fn(e, c=c, sq=sq):
                    for a in range(nh):
                        last = e.matmul(acc[:, a, 0:hw], lhsT=ones_b[:, :], rhs=sq[:, a * hw:(a + 1) * hw],
                                        start=(c == 0), stop=(c == NCH - 1))
                    return last
                P.op("pe", fn, reads=[ones_b, sq], writes=[acc])
            act(lambda e: e.activation(out=V(rstd[:, 0:T], hw), in_=acc[:, 0:nh, 0:hw], func=AF.Ln, scale=1.0 / D, bias=EPS), [acc], [rstd])
            act(lambda e: e.activation(out=rstd[:, 0:T], in_=rstd[:, 0:T], func=AF.Exp, scale=-0.5), [rstd], [rstd])
            if outs is not None:
                for c in range(NCH):
                    dve(lambda e, c=c: e.scalar_tensor_tensor(out=outs[c][:, 0:T], in0=S[c][:, 0:T], scalar=gD[:, c, gi:gi + 1],
                                                              in1=rstd[:, 0:T], op0=ALU.mult, op1=ALU.mult), [S[c], gD, rstd], [outs[c]])

        def post_norm_residual(T, gi, factor):
            sumsq_norm(Y, T, gi, Y)
            for c in range(NCH):
                dve(lambda e, c=c: e.scalar_tensor_tensor(out=R[c][:, 0:T], in0=Y[c][:, 0:T], scalar=float(factor), in1=R[c][:, 0:T],
                                                          op0=ALU.mult, op1=ALU.add), [Y[c], R[c]], [R[c]])

        def ffn(T, wg, wu, wd, gpre, gpost):
            nh, hw = halves(T)
            sumsq_norm(R, T, gpre, xn)
            for half in range(2):
                for j in range(22):
                    jj = half * 22 + j
                    wG = wreq(wg, jj, 128, 16); G = mm(wG, 16, 128, xn, T)
                    wU = wreq(wu, jj, 128, 16); U = mm(wU, 16, 128, xn, T)
                    ta = tmpA[j % 2]
                    act(lambda e, G=G, ta=ta: e.activation(out=V(ta[:, 0:T], hw), in_=G[:, 0:nh, 0:hw], func=AF.Silu), [G], [ta])
                    dve(lambda e, U=U, ta=ta, j=j: e.tensor_tensor(out=V(H[j][:, 0:T], hw), in0=V(ta[:, 0:T], hw), in1=U[:, 0:nh, 0:hw], op=ALU.mult),
                        [U, ta], [H[j]])
                for c in range(NCH):
                    wD = wreq(wd, half * 16 + c, 128, 22); A = mm(wD, 22, 128, H, T)
                    if half == 0:
                        act(lambda e, A=A, c=c: e.activation(out=V(Y[c][:, 0:T], hw), in_=A[:, 0:nh, 0:hw], func=AF.Copy), [A], [Y[c]])
                    else:
                        dve(lambda e, A=A, c=c: e.tensor_tensor(out=V(Y[c][:, 0:T], hw), in0=V(Y[c][:, 0:T], hw), in1=A[:, 0:nh, 0:hw], op=ALU.add),
                            [A, Y[c]], [Y[c]])
            post_norm_residual(T, gpost, 0.5)

        def transpose_in(src_rows_ap, nt, col0, dst_tiles, dst_tensor, ring, ident, nchunks=NCH, c0=0):
            xi = ring[transpose_in.n % 2]; transpose_in.n += 1
            P.dma("sp", xi[0:nt, 0:nchunks * 128], src_rows_ap, writes=[xi])
            for q in range(0, nchunks, 4):
                pm = next_pm()
                nq = min(4, nchunks - q)

                def fn(e, q=q, pm=pm, nq=nq):
                    for i in range(nq):
                        last = e.transpose(out=pm[:, i, 0:nt], in_=xi[0:nt, (q + i) * 128:(q + i + 1) * 128], identity=ident[0:nt, 0:nt])
                    return last
                P.op("pe", fn, reads=[xi, ident], writes=[pm])
                act(lambda e, q=q, pm=pm, nq=nq: e.activation(out=dst_tensor[:, c0 + q:c0 + q + nq, col0:col0 + nt], in_=pm[:, 0:nq, 0:nt], func=AF.Copy),
                    [pm], dst_tiles[c0 + q:c0 + q + nq])
        transpose_in.n = 0

        def transpose_out(src_tensor, src_tiles, col0, nt, dst_rows_ap, nchunks=NCH, c0=0):
            ot = otok[transpose_in.n % 2]; transpose_in.n += 1
            for q in range(0, nchunks, 4):
                pm = next_pm()
                nq = min(4, nchunks - q)

                def fn(e, q=q, pm=pm, nq=nq):
                    for i in range(nq):
                        last = e.transpose(out=pm[0:nt, i, :], in_=src_tensor[:, c0 + q + i, col0:col0 + nt], identity=identf[:, :])
                    return last
                P.op("pe", fn, reads=list(src_tiles[c0 + q:c0 + q + nq]) + [identf], writes=[pm])
                act(lambda e, q=q, pm=pm, nq=nq: e.activation(out=ot[0:nt, q * 128:(q + nq) * 128].rearrange("p (a b) -> p a b", a=nq),
                                                            in_=pm[0:nt, 0:nq, :], func=AF.Copy), [pm], [ot])
            P.dma("sp", dst_rows_ap, ot[0:nt, 0:nchunks * 128], reads=[ot])

        pool_op = lambda fn, reads, writes: P.op("pool", fn, reads=reads, writes=writes)
        pool_op(lambda e: e.memset(ones_b[:, :], 1.0), [], [ones_b])
        pool_op(lambda e: e.memset(ones_f[:, :], 1.0), [], [ones_f])
        pool_op(lambda e: e.affine_select(out=identf[:, :], in_=ones_f[:, :], pattern=[[-1, 128]], compare_op=ALU.is_equal,
                                          fill=0.0, base=0, channel_multiplier=1), [ones_f], [identf])
        dve(lambda e: e.tensor_copy(out=identb[:, :], in_=identf[:, :]), [identf], [identb])
        pool_op(lambda e: e.affine_select(out=M1p[:, :], in_=ones_f[:, :], pattern=[[1, 128]], compare_op=ALU.is_ge,
                                          fill=0.0, base=0, channel_multiplier=-1), [ones_f], [M1p])
        dve(lambda e: e.tensor_scalar(out=NEGp[:, :], in0=M1p[:, :], scalar1=-1.0, scalar2=-NEG, op0=ALU.add, op1=ALU.mult), [M1p], [NEGp])
        pool_op(lambda e: e.affine_select(out=M2s[:, :].rearrange("p (j t) -> p j t", t=8), in_=ones_f[:, :].rearrange("p (j t) -> p j t", t=8),
                                          pattern=[[8, 16], [0, 8]], compare_op=ALU.is_ge, fill=0.0, base=7, channel_multiplier=-1),
                [ones_f], [M2s])
        pool_op(lambda e: e.affine_select(out=M2s[:, :].rearrange("p (j t) -> p j t", t=8), in_=M2s[:, :].rearrange("p (j t) -> p j t", t=8),
                                          pattern=[[-8, 16], [0, 8]], compare_op=ALU.is_ge, fill=0.0, base=0, channel_multiplier=1),
                [M2s], [M2s])
        dve(lambda e: e.tensor_tensor(out=M1s[:, :], in0=M2s[:, :], in1=M1p[:, :], op=ALU.mult), [M2s, M1p], [M1s])
        dve(lambda e: e.tensor_scalar(out=NEGs[:, :], in0=M1s[:, :], scalar1=-1.0, scalar2=-NEG, op0=ALU.add, op1=ALU.mult), [M1s], [NEGs])
        dve(lambda e: e.tensor_copy(out=seqind[:, :], in_=M2s[:, 0:64].rearrange("p (j t) -> p j t", t=8)[:, :, 0]), [M2s], [seqind])
        P.dma("sp", flag[:, :], flag_d[:, :], writes=[flag])
        P.dma("sp", gH[:, 0:3], vecH.rearrange("v h -> h v"), writes=[gH])
        P.dma("sp", A_b[:, :], vecH[1:2, :].broadcast_to([128, 32]), writes=[A_b])
        P.dma("sp", D_b[:, :], vecH[2:3, :].broadcast_to([128, 32]), writes=[D_b])
        act(lambda e: e.activation(out=A_b[:, :], in_=A_b[:, :], func=AF.Exp), [A_b], [A_b])
        dve(lambda e: e.tensor_scalar_mul(out=A_b[:, :], in0=A_b[:, :], scalar1=-1.0), [A_b], [A_b])
        dve(lambda e: e.tensor_copy(out=Dcol[0:64, :], in_=D_b[0:64, :].rearrange("p (c two) -> p c two", two=2)[:, :, 0]), [D_b], [Dcol])
        dve(lambda e: e.tensor_copy(out=Dcol[64:128, :], in_=D_b[64:128, :].rearrange("p (c two) -> p c two", two=2)[:, :, 1]), [D_b, Dcol], [Dcol])
        transpose_in(vecD[:, :], 7, 0, [gD] * NCH, gD.ap, xin, identf)
        transpose_in(vec3072[:, 0:2048], 5, 0, [g3072] * 24, g3072.ap, xin, identf, nchunks=16)
        transpose_in(vec3072[:, 2048:3072], 5, 0, [g3072] * 24, g3072.ap, xin, identf, nchunks=8, c0=16)
        transpose_in(vec1024[:, :], 34, 0, [g1024] * 8, g1024.ap, xin, identf, nchunks=8)
        pool_op(lambda e: e.memset(hist_x[:, :, :], 0.0), [], [hist_x])
        pool_op(lambda e: e.memset(hist_u[:, :, :], 0.0), [], [hist_u])
        pool_op(lambda e: e.memset(hT32[:, :], 0.0), [], [hT32])

        def ssd_chunk(col0, Q, sample, do_y):
            M1 = M1s if sample else M1p
            M2 = M2s if sample else ones_f
            cs = slice(col0, col0 + Q)
            pm = next_pm()
            P.op("pe", lambda e: e.transpose(out=pm[0:Q, 0, 0:32], in_=dtF[0:32, cs], identity=identf[0:32, 0:32]), [dtF, identf], [pm])
            act(lambda e: e.activation(out=dt_tm[0:Q, :], in_=pm[0:Q, 0, 0:32], func=AF.Copy), [pm], [dt_tm])
            dve(lambda e: e.tensor_tensor(out=dtA[0:Q, :], in0=dt_tm[0:Q, :], in1=A_b[0:Q, :], op=ALU.mult), [dt_tm, A_b], [dtA])
            pm2 = next_pm()

            def fn2(e):
                e.matmul(pm2[0:Q, 0, 0:32], lhsT=M1[0:Q, 0:Q], rhs=dtA[0:Q, :], start=True, stop=True)
                return e.matmul(pm2[0:Q, 1, 0:32], lhsT=M2[0:Q, 0:Q], rhs=dtA[0:Q, :], start=True, stop=True)
            P.op("pe", fn2, [M1, M2, dtA], [pm2])
            dve(lambda e: e.tensor_scalar_mul(out=ncum[0:Q, :], in0=pm2[0:Q, 0, 0:32], scalar1=-1.0), [pm2], [ncum])
            dve(lambda e: e.tensor_tensor(out=dedt[0:Q, :], in0=pm2[0:Q, 1, 0:32], in1=ncum[0:Q, :], op=ALU.add), [pm2, ncum], [dedt])
            act(lambda e: e.activation(out=dedt[0:Q, :], in_=dedt[0:Q, :], func=AF.Exp), [dedt], [dedt])
            dve(lambda e: e.tensor_tensor(out=dedt[0:Q, :], in0=dedt[0:Q, :], in1=dt_tm[0:Q, :], op=ALU.mult), [dedt, dt_tm], [dedt])
            for q in range(0, 16, 4):
                pmx = next_pm()
                pmx_b = pmx.ap.bitcast(BF16)

                def fnx(e, q=q, pmx_b=pmx_b):
                    for i in range(4):
                        last = e.transpose(out=pmx_b[0:Q, i, 0:128], in_=H[q + i][:, cs], identity=identb[:, :])
                    return last
                P.op("pe", fnx, reads=H[q:q + 4] + [identb], writes=[pmx])
                act(lambda e, q=q, pmx_b=pmx_b: e.activation(out=xtok[0:Q, q * 128:(q + 4) * 128].rearrange("p (a b) -> p a b", a=4),
                                                             in_=pmx_b[0:Q, 0:4, 0:128], func=AF.Copy), [pmx], [xtok])
            dve(lambda e: e.tensor_tensor(out=xw[0:Q, :].rearrange("p (h d) -> p h d", d=64), in0=xtok[0:Q, :].rearrange("p (h d) -> p h d", d=64),
                                          in1=dedt[0:Q, :].unsqueeze(2).broadcast_to([Q, 32, 64]), op=ALU.mult), [xtok, dedt], [xw])
            pmb = next_pm()
            pmb_b = pmb.ap.bitcast(BF16)

            def fnb(e):
                for i in range(4):
                    last = e.transpose(out=pmb_b[0:Q, i, 0:128], in_=H[16 + i][:, cs], identity=identb[:, :])
                return last
            P.op("pe", fnb, reads=H[16:20] + [identb], writes=[pmb])
            act(lambda e: e.activation(out=btok[0:Q, :].rearrange("p (a b) -> p a b", a=4), in_=pmb_b[0:Q, 0:4, 0:128], func=AF.Copy), [pmb], [btok])
            if do_y:
                pmc = next_pm()

                def fnc(e):
                    for g in range(4):
                        last = e.matmul(pmc[0:Q, g, 0:Q], lhsT=H[16 + g][:, cs], rhs=H[20 + g][:, cs], start=True, stop=True)
                    return last
                P.op("pe", fnc, reads=H[16:24], writes=[pmc])
                act(lambda e: e.activation(out=cbt[0:Q, :, 0:Q], in_=pmc[0:Q, :, 0:Q], func=AF.Copy), [pmc], [cbt])

        def ssd_heads_prompt(col0):
            Q = 128
            cs = slice(col0, col0 + Q)
            wgv = WG.ap.rearrange("p (j t) -> p j t", t=128)
            scv = SCG.ap.rearrange("p (j t) -> p j t", t=128)
            def fcum_p(g):
                A1_ = next_acc()
                a1v_ = A1_.ap.rearrange("p a (j t) -> p (a j) t", t=128)

                def fcum(e):
                    for j in range(8):
                        h = 8 * g + j
                        last = e.matmul(a1v_[:, j, :], lhsT=dtA[0:Q, h:h + 1].broadcast_to([Q, 128]), rhs=M1p[0:Q, 0:Q], start=True, stop=True)
                    return last
                P.op("pe", fcum, [dtA, M1p], [A1_])
                return A1_, a1v_
            nxt = fcum_p(0)
            for g in range(4):
                A1, a1v = nxt
                A2 = next_acc()
                a2v = A2.ap.rearrange("p a (j t) -> p (a j) t", t=128)
                hs = slice(8 * g, 8 * g + 8)
                if g + 1 < 4:
                    nxt = fcum_p(g + 1)
                dve(lambda e, a1v=a1v, a2v=a2v, hs=hs: e.tensor_tensor(out=a2v, in0=a1v, in1=ncum[0:Q, hs].unsqueeze(2).broadcast_to([Q, 8, 128]), op=ALU.add),
                    [A1, ncum], [A2])
                dve(lambda e, a2v=a2v: e.tensor_tensor(out=a2v, in0=a2v, in1=NEGp[:, :].unsqueeze(1).broadcast_to([128, 8, 128]), op=ALU.add), [A2, NEGp], [A2])
                act(lambda e, a2v=a2v: e.activation(out=a2v, in_=a2v, func=AF.Exp), [A2], [A2])
                act(lambda e, a1v=a1v: e.activation(out=a1v, in_=a1v, func=AF.Exp), [A1], [A1])
                dve(lambda e, a2v=a2v, hs=hs: e.tensor_tensor(out=a2v, in0=a2v, in1=dt_tm[0:Q, hs].unsqueeze(2).broadcast_to([Q, 8, 128]), op=ALU.mult),
                    [A2, dt_tm], [A2])
                dve(lambda e, a2v=a2v, g=g: e.tensor_tensor(out=wgv, in0=a2v, in1=cbt[0:Q, g, 0:Q].unsqueeze(1).broadcast_to([Q, 8, Q]), op=ALU.mult),
                    [A2, cbt], [WG])
                dve(lambda e, a1v=a1v, g=g: e.tensor_tensor(out=scv, in0=a1v, in1=H[20 + g][:, cs].unsqueeze(1).broadcast_to([128, 8, Q]), op=ALU.mult),
                    [A1, H[20 + g]], [SCG])
                for jp in range(4):
                    pr = 4 * g + jp
                    pmy = next_pm()

                    def fny(e, pmy=pmy, jp=jp, g=g):
                        for hh in range(2):
                            j = 2 * jp + hh
                            h = 8 * g + j
                            e.matmul(pmy[64 * hh:64 * hh + 64, 3, 0:Q], lhsT=xtok[0:Q, h * 64:(h + 1) * 64], rhs=wgv[0:Q, j, :], start=True, stop=False)
                            last = e.matmul(pmy[64 * hh:64 * hh + 64, 3, 0:Q], lhsT=hTb[:, h * 64:(h + 1) * 64], rhs=scv[:, j, :], start=False, stop=True)
                        return last
                    P.op("pe", fny, [xtok, WG, SCG, hTb], [pmy])
                    dve(lambda e, pr=pr, pmy=pmy: e.scalar_tensor_tensor(out=Y[pr][:, cs], in0=H[pr][:, cs], scalar=Dcol[:, pr:pr + 1],
                                                                         in1=pmy[:, 3, 0:Q], op0=ALU.mult, op1=ALU.add), [H[pr], Dcol, pmy], [Y[pr]])

        def state_update(Q, bt_tile, s32, sb16, jcol=None):
            pmd = next_pm()
            if jcol is None:
                P.op("pe", lambda e: e.matmul(pmd[:, 0, 0:32], lhsT=ones_f[0:Q, :], rhs=dtA[0:Q, :], start=True, stop=True), [ones_f, dtA], [pmd])
            else:
                P.op("pe", lambda e: e.matmul(pmd[:, 0, 0:32], lhsT=seqind[0:Q, jcol:jcol + 1].broadcast_to([Q, 128]), rhs=dtA[0:Q, :], start=True, stop=True),
                     [seqind, dtA], [pmd])
            act(lambda e: e.activation(out=decb[:, :], in_=pmd[:, 0, 0:32], func=AF.Exp), [pmd], [decb])
            dve(lambda e: e.tensor_tensor(out=s32[:, :].rearrange("p (h d) -> p h d", d=64), in0=s32[:, :].rearrange("p (h d) -> p h d", d=64),
                                          in1=decb[:, :].unsqueeze(2).broadcast_to([128, 32, 64]), op=ALU.mult), [s32, decb], [s32])
            for g in range(4):
                a = next_acc()
                P.op("pe", lambda e, g=g, a=a: e.matmul(a[:, 0, :], lhsT=bt_tile[0:Q, g * 128:(g + 1) * 128], rhs=xw[0:Q, g * 512:(g + 1) * 512],
                                                        start=True, stop=True), [bt_tile, xw], [a])
                dve(lambda e, g=g, a=a: e.tensor_tensor(out=s32[:, g * 512:(g + 1) * 512], in0=s32[:, g * 512:(g + 1) * 512], in1=a[:, 0, :], op=ALU.add),
                    [a, s32], [s32])
            if sb16 is not None:
                act(lambda e: e.activation(out=sb16[:, :], in_=s32[:, :], func=AF.Copy), [s32], [sb16])

        def store_state_nat(s32, dst_ap):
            sn_t = xin[transpose_in.n % 2]; transpose_in.n += 1
            snv = sn_t.ap.rearrange("p (a n) -> p a n", a=16)
            for q in range(0, 16, 4):
                pm = next_pm()

                def fn(e, q=q, pm=pm):
                    for i in range(4):
                        last = e.transpose(out=pm[:, i, :], in_=s32[:, (q + i) * 128:(q + i + 1) * 128], identity=identf[:, :])
                    return last
                P.op("pe", fn, [s32, identf], [pm])
                act(lambda e, q=q, pm=pm: e.activation(out=snv[:, q:q + 4, :], in_=pm[:, :, :], func=AF.Copy), [pm], [sn_t])
            P.dma("sp", dst_ap.rearrange("(a p) n -> p a n", p=128), snv[:, :, :], reads=[sn_t])

        def sample_ssd(gidx):
            Q = 64
            cs = slice(512, 576)
            ssd_chunk(512, Q, True, True)
            ys = next_acc()
            reserved[0] = ys
            ysv = ys.ap.rearrange("p a (c t) -> p (a c) t", t=64)
            scall = hTb
            def fclr(e):
                e.matmul(ys[:, 0, :], lhsT=NEGp[0:1, :], rhs=R_t[0:1, 0, 0:512], start=True, stop=False, skip_group_check=True)
                return e.matmul(ys[:, 1, :], lhsT=NEGp[0:1, :], rhs=R_t[0:1, 0, 0:512], start=True, stop=False, skip_group_check=True)
            P.op("pe", fclr, [NEGp, R[0]], [ys])
            scv = scall.ap.rearrange("p (h t) -> p h t", t=64)
            def fcum_s(g):
                A1_ = next_acc()
                a1v_ = A1_[:, 0, :].rearrange("p (j t) -> p j t", t=64)

                def fcum(e):
                    for j in range(8):
                        h = 8 * g + j
                        last = e.matmul(a1v_[:, j, :], lhsT=dtA[0:Q, h:h + 1].broadcast_to([Q, 128]), rhs=M1s[0:Q, 0:Q], start=True, stop=True)
                    return last
                P.op("pe", fcum, [dtA, M1s], [A1_])
                return A1_, a1v_
            for g in range(4):
                A1, a1v = fcum_s(g)
                A2 = next_acc()
                a2v = A2[:, 0, :].rearrange("p (j t) -> p j t", t=64)
                hs = slice(8 * g, 8 * g + 8)
                dve(lambda e, a1v=a1v, a2v=a2v, hs=hs: e.tensor_tensor(out=a2v[0:Q], in0=a1v[0:Q], in1=ncum[0:Q, hs].unsqueeze(2).broadcast_to([Q, 8, Q]), op=ALU.add),
                    [A1, ncum], [A2])
                dve(lambda e, a2v=a2v: e.tensor_tensor(out=a2v[0:Q], in0=a2v[0:Q], in1=NEGs[0:Q, 0:Q].unsqueeze(1).broadcast_to([Q, 8, Q]), op=ALU.add), [A2, NEGs], [A2])
                act(lambda e, a2v=a2v: e.activation(out=a2v[0:Q], in_=a2v[0:Q], func=AF.Exp), [A2], [A2])
                act(lambda e, a1v=a1v: e.activation(out=a1v, in_=a1v, func=AF.Exp), [A1], [A1])
                dve(lambda e, a2v=a2v, hs=hs: e.tensor_tensor(out=a2v[0:Q], in0=a2v[0:Q], in1=dt_tm[0:Q, hs].unsqueeze(2).broadcast_to([Q, 8, Q]), op=ALU.mult),
                    [A2, dt_tm], [A2])
                dve(lambda e, a2v=a2v, g=g: e.tensor_tensor(out=wS[0:Q, :, :], in0=a2v[0:Q], in1=cbt[0:Q, g, 0:Q].unsqueeze(1).broadcast_to([Q, 8, Q]), op=ALU.mult),
                    [A2, cbt], [wS])
                dve(lambda e, a1v=a1v, g=g: e.tensor_tensor(out=scv[:, 8 * g:8 * g + 8, :], in0=a1v, in1=H[20 + g][:, cs].unsqueeze(1).broadcast_to([128, 8, Q]), op=ALU.mult),
                    [A1, H[20 + g]], [scall])

                def fyd(e, g=g):
                    for j in range(8):
                        h = 8 * g + j
                        pr, hh = h // 2, h % 2
                        last = e.matmul(ysv[64 * hh:64 * hh + 64, pr, 0:Q], lhsT=xtok[0:Q, h * 64:(h + 1) * 64], rhs=wS[0:Q, j, :],
                                        start=False, stop=False, skip_group_check=True)
                    return last
                P.op("pe", fyd, [xtok, wS], [ys])
            for j in range(8):
                seq = gidx * 8 + j
                sn_t = xin[transpose_in.n % 2]; transpose_in.n += 1
                so_t = xin[transpose_in.n % 2]; transpose_in.n += 1
                snv = sn_t.ap.rearrange("p (a n) -> p a n", a=16)
                sov = so_t.ap.rearrange("p (a n) -> p a n", a=16)
                P.dma("sp", snv[:, :, :], st_ssm[seq].rearrange("(a p) n -> p a n", p=128), writes=[sn_t])
                for q in range(0, 16, 4):
                    pm = next_pm()

                    def fn(e, q=q, pm=pm, snv=snv):
                        for i in range(4):
                            last = e.transpose(out=pm[:, i, :], in_=snv[:, q + i, :], identity=identf[:, :])
                        return last
                    P.op("pe", fn, [sn_t, identf], [pm])
                    act(lambda e, q=q, pm=pm: e.activation(out=sTb[:, q * 128:(q + 4) * 128].rearrange("p (a b) -> p a b", a=4), in_=pm[:, :, :], func=AF.Copy),
                        [pm], [sTb])

                def fno(e, j=j):
                    for h in range(32):
                        pr, hh = h // 2, h % 2
                        last = e.matmul(ysv[64 * hh:64 * hh + 64, pr, 8 * j:8 * j + 8], lhsT=sTb[:, h * 64:(h + 1) * 64],
                                        rhs=scall[:, h * 64 + 8 * j:h * 64 + 8 * j + 8], start=False, stop=(j == 7), skip_group_check=True)
                    return last
                P.op("pe", fno, [sTb, scall], [ys])
                dve(lambda e, j=j: e.tensor_scalar(out=btm[0:Q, :], in0=btok[0:Q, :], scalar1=seqind[0:Q, j:j + 1], scalar2=None, op0=ALU.mult),
                    [btok, seqind], [btm])
                pmd = next_pm()
                P.op("pe", lambda e, j=j, pmd=pmd: e.matmul(pmd[:, 0, 0:32], lhsT=seqind[0:Q, j:j + 1].broadcast_to([Q, 128]), rhs=dtA[0:Q, :], start=True, stop=True),
                     [seqind, dtA], [pmd])
                act(lambda e, pmd=pmd: e.activation(out=decb[:, :], in_=pmd[:, 0, 0:32], func=AF.Exp), [pmd], [decb])
                dve(lambda e: e.tensor_copy(out=dnat[0:64, :], in_=decb[0:64, :].rearrange("p (c two) -> p c two", two=2)[:, :, 0]), [decb], [dnat])
                dve(lambda e: e.tensor_copy(out=dnat[64:128, :], in_=decb[64:128, :].rearrange("p (c two) -> p c two", two=2)[:, :, 1]), [decb, dnat], [dnat])
                for q in range(4):
                    pm = next_pm()

                    def fup(e, q=q, pm=pm):
                        for i in range(4):
                            t = 4 * q + i
                            last = e.matmul(pm[:, i, :], lhsT=xw[0:Q, t * 128:(t + 1) * 128], rhs=btm[0:Q, q * 128:(q + 1) * 128], start=True, stop=True)
                        return last
                    P.op("pe", fup, [xw, btm], [pm])
                    for i in range(4):
                        t = 4 * q + i
                        dve(lambda e, t=t, i=i, pm=pm, snv=snv, sov=sov: e.scalar_tensor_tensor(out=sov[:, t, :], in0=snv[:, t, :], scalar=dnat[:, t:t + 1],
                                                                                              in1=pm[:, i, :], op0=ALU.mult, op1=ALU.add),
                            [sn_t, dnat, pm], [so_t])
                P.dma("sp", o_ssm_s[seq].rearrange("(a p) n -> p a n", p=128), sov[:, :, :], reads=[so_t])
            reserved[0] = None
            for pr in range(16):
                dve(lambda e, pr=pr: e.scalar_tensor_tensor(out=Y[pr][:, cs], in0=H[pr][:, cs], scalar=Dcol[:, pr:pr + 1],
                                                            in1=ysv[:, pr, :], op0=ALU.mult, op1=ALU.add), [H[pr], Dcol, ys], [Y[pr]])

        groups = [("pre", 0), ("pre", 1), ("main", 0), ("main", 1)]
        def group_body(gi_, kind, gidx):
            main = kind == "main"
            T = 576 if main else 512
            nh, hw = halves(T)
            xsrc = xmain if main else xpre
            for tt in range(4):
                transpose_in(xsrc[gidx * 512 + tt * 128: gidx * 512 + (tt + 1) * 128, :], 128, tt * 128, R, R_t, xin, identf)
            if main:
                transpose_in(xsam[gidx * 64:(gidx + 1) * 64, :], 64, 512, R, R_t, xin, identf)
            ffn(T, f1g, f1u, f1d, 0, 1)
            if dbg_stop == "ffn1":
                for tt in range(4):
                    transpose_out(R_t, R, tt * 128, 128, y_main[tt * 128:(tt + 1) * 128, :])
                return True
            sumsq_norm(R, T, 2, xn)
            def xbc_mm(c):
                wt = wreq(win_xbc, c, 128, 16)
                return mm(wt, 16, 128, xn, T)
            def xbc_evac(c, A):
                sg = stg[c % 2]; ss = sstg[c % 2]
                dve(lambda e: e.tensor_copy(out=sg[:, 29:32], in_=hist_x[:, c, :]), [hist_x], [sg])
                act(lambda e: e.activation(out=sg[:, 32:32 + hw], in_=A[:, 0, 0:hw], func=AF.Copy), [A], [sg])
                if nh == 2:
                    act(lambda e: e.activation(out=sg[:, 32 + hw:32 + 512], in_=A[:, 1, 0:512 - hw], func=AF.Copy), [A], [sg])
                dve(lambda e: e.tensor_copy(out=hist_x[:, c, :], in_=sg[:, 541:544]), [sg], [hist_x])

            def xbc_conv(c, A):
                sg = stg[c % 2]; ss = sstg[c % 2]; ca = cacc[c % 2]
                if main:
                    ci = cin[cin_n[0] % 2]; cin_n[0] += 1
                    P.dma("sp", ci[0:24, 0, :], st_sconv[gidx * 24:(gidx + 1) * 24, c * 128:(c + 1) * 128], writes=[ci])
                    pmq = next_pm()
                    P.op("pe", lambda e: e.transpose(out=pmq[:, 0, 0:24], in_=ci[0:24, 0, :], identity=identf[0:24, 0:24]), [ci, identf], [pmq])
                    act(lambda e: e.activation(out=ss[:, :, 0:3], in_=pmq[:, 0, 0:24].rearrange("p (j t) -> p j t", t=3), func=AF.Copy), [pmq], [ss])
                    act(lambda e: e.activation(out=ss[:, :, 3:11], in_=A[:, 1, 512 - hw:hw].rearrange("p (j t) -> p j t", t=8), func=AF.Copy), [A], [ss])
                dve(lambda e: e.tensor_scalar(out=ca[:, 0:512], in0=sg[:, 29:29 + 512], scalar1=g3072[:, c, 0:1], scalar2=None, op0=ALU.mult), [sg, g3072], [ca])
                for k in range(1, 4):
                    dve(lambda e, k=k: e.scalar_tensor_tensor(out=ca[:, 0:512], in0=sg[:, 29 + k:29 + k + 512], scalar=g3072[:, c, k:k + 1],
                                                              in1=ca[:, 0:512], op0=ALU.mult, op1=ALU.add), [sg, g3072, ca], [ca])
                if main:
                    pmo = next_pm()
                    ct = cin[cin_n[0] % 2]; cin_n[0] += 1
                    ctv = ct.ap.rearrange("p a b -> p (a b)")
                    dve(lambda e: e.tensor_copy(out=ctv[:, 0:24].rearrange("p (j t) -> p j t", t=3), in_=ss[:, :, 8:11]), [ss], [ct])
                    P.op("pe", lambda e: e.transpose(out=pmo[0:24, 0, :], in_=ctv[:, 0:24], identity=identf[:, :]), [ct, identf], [pmo])
                    co = cin[cin_n[0] % 2]; cin_n[0] += 1
                    act(lambda e: e.activation(out=co[0:24, 0, :], in_=pmo[0:24, 0, :], func=AF.Copy), [pmo], [co])
                    P.dma("sp", o_sconv_s[gidx * 24:(gidx + 1) * 24, c * 128:(c + 1) * 128], co[0:24, 0, :], reads=[co])
                    cav = ca[:, 512:576].rearrange("p (j t) -> p j t", t=8)
                    dve(lambda e: e.tensor_scalar(out=cav, in0=ss[:, :, 0:8], scalar1=g3072[:, c, 0:1], scalar2=None, op0=ALU.mult), [ss, g3072], [ca])
                    for k in range(1, 4):
                        dve(lambda e, k=k: e.scalar_tensor_tensor(out=cav, in0=ss[:, :, k:k + 8], scalar=g3072[:, c, k:k + 1],
                                                                  in1=cav, op0=ALU.mult, op1=ALU.add), [ss, g3072, ca], [ca])
                act(lambda e: e.activation(out=H[c][:, 0:T], in_=ca[:, 0:T], func=AF.Silu, bias=g3072[:, c, 4:5]), [ca, g3072], [H[c]])

            accs = {0: xbc_mm(0), 1: xbc_mm(1)}
            xbc_evac(0, accs[0])
            for c in range(24):
                if c + 2 < 24:
                    accs[c + 2] = xbc_mm(c + 2)
                if c + 1 < 24:
                    xbc_evac(c + 1, accs[c + 1])
                xbc_conv(c, accs[c])
            if dbg_stop == "xbc2" and main:
                for c in range(NCH):
                    dve(lambda e, c=c: e.tensor_copy(out=Y[c][:, 0:T], in_=H[c][:, 0:T]), [H[c]], [Y[c]])
                for tt in range(4):
                    transpose_out(Y_t, Y, tt * 128, 128, y_main[tt * 128:(tt + 1) * 128, :])
                transpose_out(Y_t, Y, 512, 64, y_sam[0:64, :])
                return True
            wt = wreq(win_dt, 0, 32, 16); A = mm(wt, 16, 32, xn, T)
            act(lambda e, A=A: e.activation(out=V(dtF[0:32, 0:T], hw), in_=A[0:32, 0:nh, 0:hw], func=AF.Exp, bias=gH[0:32, 0:1]), [A, gH], [dtF])
            act(lambda e: e.activation(out=dtF[0:32, 0:T], in_=dtF[0:32, 0:T], func=AF.Ln, bias=1.0), [dtF], [dtF])
            fence(MG + ALIAS)
            act(lambda e: e.activation(out=hTb[:, :], in_=hT32[:, :], func=AF.Copy), [hT32], [hTb])
            if not main and gidx == 1:
                pass
            for ch in range(4):
                ssd_chunk(ch * 128, 128, False, main)
                if main:
                    ssd_heads_prompt(ch * 128)
                state_update(128, btok, hT32, hTb)
            if dbg_stop == "ssd" and gi_ == 0:
                store_state_nat(hT32, o_ssm_p)
                return True
            if not main:
                if gidx == 1:
                    dve(lambda e: e.tensor_scalar(out=hT32[:, :], in0=hT32[:, :], scalar1=flag[:, 0:1], scalar2=None, op0=ALU.mult), [hT32, flag], [hT32])
                    for c in range(8):
                        wa = wreq(win_glu, c, 128, 16); wb = wreq(win_glu, 8 + c, 128, 16)
                        a = next_acc()

                        def fng(e, wa=wa, wb=wb, a=a):
                            for k in range(16):
                                e.matmul(a[:, 0, 0:32], lhsT=wa[:, k, :], rhs=xn[k][:, 480:512], start=(k == 0), stop=(k == 15))
                            for k in range(16):
                                last = e.matmul(a[:, 1, 0:32], lhsT=wb[:, k, :], rhs=xn[k][:, 480:512], start=(k == 0), stop=(k == 15))
                            return last
                        P.op("pe", fng, [wa, wb] + xn, [a])
                        ta = tmpA[0]
                        act(lambda e, a=a, ta=ta: e.activation(out=ta[:, 0:32], in_=a[:, 1, 0:32], func=AF.Sigmoid), [a], [ta])
                        dve(lambda e, a=a, ta=ta, c=c: e.tensor_tensor(out=hist_u[:, c, :], in0=ta[:, 2:32], in1=a[:, 0, 2:32], op=ALU.mult), [a, ta], [hist_u])
                    if dbg_stop == "pre":
                        store_state_nat(hT32, o_ssm_p)
                        transpose_out(hist_x.ap, [hist_x] * 24, 0, 3, o_sconv_p[:, 0:2048], nchunks=16)
                        transpose_out(hist_x.ap, [hist_x] * 24, 0, 3, o_sconv_p[:, 2048:3072], nchunks=8, c0=16)
                        transpose_out(hist_u.ap, [hist_u] * 8, 0, 30, o_cc_p[:, :], nchunks=8)
                        return True
                return False
            if gidx == 1:
                store_state_nat(hT32, o_ssm_p)
                transpose_out(hist_x.ap, [hist_x] * 24, 0, 3, o_sconv_p[:, 0:2048], nchunks=16)
                transpose_out(hist_x.ap, [hist_x] * 24, 0, 3, o_sconv_p[:, 2048:3072], nchunks=8, c0=16)
            fence([sTb, WG, SCG])
            sample_ssd(gidx)
            if dbg_stop == "y":
                for tt in range(4):
                    transpose_out(Y_t, Y, tt * 128, 128, y_main[tt * 128:(tt + 1) * 128, :])
                transpose_out(Y_t, Y, 512, 64, y_sam[0:64, :])
                return True
            for c in range(NCH):
                wt = wreq(win_z, c, 128, 16); A = mm(wt, 16, 128, xn, T)
                ta = tmpA[c % 2]
                act(lambda e, A=A, ta=ta: e.activation(out=V(ta[:, 0:T], hw), in_=A[:, 0:nh, 0:hw], func=AF.Silu), [A], [ta])
                dve(lambda e, ta=ta, c=c: e.tensor_tensor(out=Y[c][:, 0:T], in0=Y[c][:, 0:T], in1=ta[:, 0:T], op=ALU.mult), [ta, Y[c]], [Y[c]])
            sumsq_norm(Y, T, 6, H[0:16])
            if dbg_stop == "yn":
                dve(lambda e: e.memset(dummy[:, 1:2], 0.0), [], [dummy])
                for c in range(NCH):
                    dve(lambda e, c=c: e.tensor_copy(out=Y[c][:, 0:T], in_=H[c][:, 0:T]), [H[c]], [Y[c]])
                for tt in range(4):
                    transpose_out(Y_t, Y, tt * 128, 128, y_main[tt * 128:(tt + 1) * 128, :])
                transpose_out(Y_t, Y, 512, 64, y_sam[0:64, :])
                return True
            def glu_mm(c):
                wa = wreq(win_glu, c, 128, 16); GA_ = mm(wa, 16, 128, xn, T)
                wb = wreq(win_glu, 8 + c, 128, 16); GB_ = mm(wb, 16, 128, xn, T)
                return GA_, GB_
            G_next = glu_mm(0)
            for c in range(8):
                GA, GB = G_next
                ta = tmpA[c % 2]; sg = stg[c % 2]; ss = sstg[0]
                act(lambda e, GB=GB, ta=ta: e.activation(out=V(ta[:, 0:T], hw), in_=GB[:, 0:nh, 0:hw], func=AF.Sigmoid), [GB], [ta])
                dve(lambda e, sg=sg, c=c: e.tensor_copy(out=sg[:, 2:32], in_=hist_u[:, c, :]), [hist_u], [sg])
                dve(lambda e, sg=sg, GA=GA, ta=ta: e.tensor_tensor(out=sg[:, 32:32 + hw], in0=GA[:, 0, 0:hw], in1=ta[:, 0:hw], op=ALU.mult), [GA, ta], [sg])
                dve(lambda e, sg=sg, GA=GA, ta=ta: e.tensor_tensor(out=sg[:, 32 + hw:32 + 512], in0=GA[:, 1, 0:512 - hw], in1=ta[:, hw:512], op=ALU.mult), [GA, ta], [sg])
                dve(lambda e, sg=sg, c=c: e.tensor_copy(out=hist_u[:, c, :], in_=sg[:, 514:544]), [sg], [hist_u])
                dve(lambda e, ss=ss, GA=GA, ta=ta: e.tensor_tensor(out=ss[:, :, 30:38], in0=GA[:, 1, 512 - hw:hw].rearrange("p (j t) -> p j t", t=8),
                                                                 in1=ta[:, 512:576].rearrange("p (j t) -> p j t", t=8), op=ALU.mult), [GA, ta], [ss])
                if c + 1 < 8:
                    G_next = glu_mm(c + 1)
                ci = cin[cin_n[0] % 2]; cin_n[0] += 1
                P.dma("sp", ci[0:120, :, :], st_cc[gidx * 240:(gidx + 1) * 240, c * 128:(c + 1) * 128].rearrange("(q r) n -> r q n", q=2), writes=[ci])
                pmq = next_pm()

                def fnq(e, ci=ci, pmq=pmq):
                    e.transpose(out=pmq[:, 0, 0:120], in_=ci[0:120, 0, :], identity=identf[0:120, 0:120])
                    return e.transpose(out=pmq[:, 1, 0:120], in_=ci[0:120, 1, :], identity=identf[0:120, 0:120])
                P.op("pe", fnq, [ci, identf], [pmq])
                for q in range(2):
                    act(lambda e, ss=ss, pmq=pmq, q=q: e.activation(out=ss[:, 4 * q:4 * q + 4, 0:30], in_=pmq[:, q, 0:120].rearrange("p (j t) -> p j t", t=30), func=AF.Copy), [pmq], [ss])
                pmo = next_pm()

                ct = cin[cin_n[0] % 2]; cin_n[0] += 1
                ctv = ct.ap.rearrange("p a b -> p (a b)")
                dve(lambda e, ss=ss, ctv=ctv: e.tensor_copy(out=ctv[:, 0:240].rearrange("p (j t) -> p j t", t=30), in_=ss[:, :, 8:38]), [ss], [ct])

                def fno2(e, ctv=ctv, pmo=pmo):
                    e.transpose(out=pmo[0:120, 0, :], in_=ctv[:, 0:120], identity=identf[:, :])
                    return e.transpose(out=pmo[0:120, 1, :], in_=ctv[:, 120:240], identity=identf[:, :])
                P.op("pe", fno2, [ct, identf], [pmo])
                co = cin[cin_n[0] % 2]; cin_n[0] += 1
                act(lambda e, co=co, pmo=pmo: e.activation(out=co[0:120, :, :], in_=pmo[0:120, 0:2, :], func=AF.Copy), [pmo], [co])
                P.dma("sp", o_cc_s[gidx * 240:(gidx + 1) * 240, c * 128:(c + 1) * 128].rearrange("(q r) n -> r q n", q=2), co[0:120, :, :], reads=[co])
                yv = Y[c][:, 512:576].rearrange("p (j t) -> p j t", t=8)
                dve(lambda e, sg=sg, c=c: e.tensor_scalar(out=Y[c][:, 0:512], in0=sg[:, 2:2 + 512], scalar1=g1024[:, c, 0:1], scalar2=g1024[:, c, 31:32],
                                                          op0=ALU.mult, op1=ALU.add), [sg, g1024], [Y[c]])
                dve(lambda e, ss=ss, c=c, yv=yv: e.tensor_scalar(out=yv, in0=ss[:, :, 0:8], scalar1=g1024[:, c, 0:1], scalar2=g1024[:, c, 31:32],
                                                                 op0=ALU.mult, op1=ALU.add), [ss, g1024], [Y[c]])
                for k in range(1, 31):
                    dve(lambda e, sg=sg, c=c, k=k: e.scalar_tensor_tensor(out=Y[c][:, 0:512], in0=sg[:, 2 + k:2 + k + 512], scalar=g1024[:, c, k:k + 1],
                                                                         in1=Y[c][:, 0:512], op0=ALU.mult, op1=ALU.add), [sg, g1024, Y[c]], [Y[c]])
                    dve(lambda e, ss=ss, c=c, k=k, yv=yv: e.scalar_tensor_tensor(out=yv, in0=ss[:, :, k:k + 8], scalar=g1024[:, c, k:k + 1],
                                                                                in1=yv, op0=ALU.mult, op1=ALU.add), [ss, g1024, Y[c]], [Y[c]])
            if gidx == 1:
                transpose_out(hist_u.ap, [hist_u] * 8, 0, 30, o_cc_p[:, :], nchunks=8)
            a1 = next_acc(); a2 = next_acc()
            for c in range(8):
                def f1(e, c=c):
                    for a in range(nh):
                        last = e.matmul(a1[:, a, 0:hw], lhsT=ones_f[:, :], rhs=Y[c][:, a * hw:(a + 1) * hw], start=(c == 0), stop=(c == 7))
                    return last
                P.op("pe", f1, [ones_f, Y[c]], [a1])
                ta = tmpA[0]
                act(lambda e, c=c, ta=ta: e.activation(out=ta[:, 0:T], in_=Y[c][:, 0:T], func=AF.Square), [Y[c]], [ta])

                def f2(e, c=c, ta=ta):
                    for a in range(nh):
                        last = e.matmul(a2[:, a, 0:hw], lhsT=ones_f[:, :], rhs=ta[:, a * hw:(a + 1) * hw], start=(c == 0), stop=(c == 7))
                    return last
                P.op("pe", f2, [ones_f, ta], [a2])
            act(lambda e: e.activation(out=V(aux[:, 0:T], hw), in_=a1[:, 0:nh, 0:hw], func=AF.Copy, scale=1.0 / 1024), [a1], [aux])
            ta = tmpA[0]
            dve(lambda e: e.tensor_tensor(out=ta[:, 0:T], in0=aux[:, 0:T], in1=aux[:, 0:T], op=ALU.mult), [aux], [ta])
            dve(lambda e: e.scalar_tensor_tensor(out=V(rstd[:, 0:T], hw), in0=a2[:, 0:nh, 0:hw], scalar=1.0 / 1024, in1=V(ta[:, 0:T], hw),
                                                 op0=ALU.mult, op1=ALU.subtract), [a2, ta], [rstd])
            act(lambda e: e.activation(out=rstd[:, 0:T], in_=rstd[:, 0:T], func=AF.Ln, bias=EPS), [rstd], [rstd])
            act(lambda e: e.activation(out=rstd[:, 0:T], in_=rstd[:, 0:T], func=AF.Exp, scale=-0.5), [rstd], [rstd])
            for c in range(8):
                dve(lambda e, c=c: e.tensor_tensor(out=ta[:, 0:T], in0=Y[c][:, 0:T], in1=aux[:, 0:T], op=ALU.subtract), [Y[c], aux], [ta])
                dve(lambda e: e.tensor_tensor(out=ta[:, 0:T], in0=ta[:, 0:T], in1=rstd[:, 0:T], op=ALU.mult), [ta, rstd], [ta])
                act(lambda e, c=c: e.activation(out=H[16 + c][:, 0:T], in_=ta[:, 0:T], func=AF.Silu, scale=g1024[:, c, 32:33], bias=g1024[:, c, 33:34]),
                    [ta, g1024], [H[16 + c]])
            if dbg_stop == "cc":
                for c in range(8):
                    dve(lambda e, c=c: e.tensor_copy(out=Y[c][:, 0:T], in_=H[16 + c][:, 0:T]), [H[16 + c]], [Y[c]])
                for tt in range(4):
                    transpose_out(Y_t, Y, tt * 128, 128, y_main[tt * 128:(tt + 1) * 128, 0:1024], nchunks=8)
                transpose_out(Y_t, Y, 512, 64, y_sam[0:64, 0:1024], nchunks=8)
                return True
            fence(MG + ALIAS)
            for c in range(NCH):
                ws = wreq(w_sso, c, 128, 16); SA = mm(ws, 16, 128, H[0:16], T)
                wg_ = wreq(win_gate, c, 128, 16); GS = mm(wg_, 16, 128, xn, T)
                ta = tmpA[c % 2]
                act(lambda e, GS=GS, ta=ta: e.activation(out=V(ta[:, 0:T], hw), in_=GS[:, 0:nh, 0:hw], func=AF.Sigmoid), [GS], [ta])
                dve(lambda e, SA=SA, ta=ta: e.tensor_tensor(out=V(ta[:, 0:T], hw), in0=V(ta[:, 0:T], hw), in1=SA[:, 0:nh, 0:hw], op=ALU.mult), [SA, ta], [ta])
                wc = wreq(w_cco, c, 128, 8); CA = mm(wc, 8, 128, H[16:24], T)
                wg2 = wreq(win_gate, 16 + c, 128, 16); GC = mm(wg2, 16, 128, xn, T)
                act(lambda e, GC=GC: e.activation(out=V(rstd[:, 0:T], hw), in_=GC[:, 0:nh, 0:hw], func=AF.Sigmoid), [GC], [rstd])
                dve(lambda e, CA=CA: e.tensor_tensor(out=V(rstd[:, 0:T], hw), in0=V(rstd[:, 0:T], hw), in1=CA[:, 0:nh, 0:hw], op=ALU.mult), [CA, rstd], [rstd])
                dve(lambda e, c=c, ta=ta: e.tensor_tensor(out=MG[c][:, 0:T], in0=ta[:, 0:T], in1=rstd[:, 0:T], op=ALU.add), [ta, rstd], [MG[c]])
            for c in range(NCH):
                wo_ = wreq(w_o, c, 128, 16); A = mm(wo_, 16, 128, MG, T)
                act(lambda e, A=A, c=c: e.activation(out=V(Y[c][:, 0:T], hw), in_=A[:, 0:nh, 0:hw], func=AF.Copy), [A], [Y[c]])
            post_norm_residual(T, 3, 1.0)
            if dbg_stop == "mix":
                for tt in range(4):
                    transpose_out(R_t, R, tt * 128, 128, y_main[tt * 128:(tt + 1) * 128, :])
                transpose_out(R_t, R, 512, 64, y_sam[0:64, :])
                return True
            ffn(T, f2g, f2u, f2d, 4, 5)
            for tt in range(4):
                transpose_out(R_t, R, tt * 128, 128, y_main[gidx * 512 + tt * 128: gidx * 512 + (tt + 1) * 128, :])
            transpose_out(R_t, R, 512, 64, y_sam[gidx * 64:(gidx + 1) * 64, :])
        for gi_, (kind, gidx) in enumerate(groups):
            if group_body(gi_, kind, gidx):
                break
        P.emit()
    return nc


def _prep_inputs(inp):
    f = lambda a: np.ascontiguousarray(np.asarray(a, dtype=np.float32))
    xp = f(inp["x_prompt"]); xs = f(inp["x_sample"])
    vecD = np.stack([f(inp[k])[0] for k in ("ffn1_pre_norm", "ffn1_post_norm", "mix_pre_norm", "mix_post_norm",
                                            "ffn2_pre_norm", "ffn2_post_norm", "ssm_norm")])
    vec3072 = np.concatenate([f(inp["ssm_conv_w"])[0], f(inp["ssm_conv_b"])], axis=0)
    vec1024 = np.concatenate([f(inp["cc_conv_w"])[0], f(inp["cc_conv_b"]), f(inp["cc_ln_g"]), f(inp["cc_ln_b"])], axis=0)
    vecH = np.concatenate([f(inp["ssm_dt_bias"]), f(inp["ssm_A_log"]), f(inp["ssm_D"])], axis=0)
    def tile_w(W, KC, ncols=128):
        NC = W.shape[1] // ncols
        return np.ascontiguousarray(W.reshape(KC, 128, NC, ncols).transpose(2, 1, 0, 3))

    def tile_down(W):
        return np.concatenate([tile_w(W[h * 2816:(h + 1) * 2816], 22) for h in range(2)], axis=0)
    w_in_full = f(inp["w_in"])[0]
    shared = dict(vecD=vecD, vec3072=vec3072, vec1024=vec1024, vecH=vecH,
                  f1g=tile_w(f(inp["ffn1_w_gate"])[0], 16), f1u=tile_w(f(inp["ffn1_w_up"])[0], 16), f1d=tile_down(f(inp["ffn1_w_down"])[0]),
                  f2g=tile_w(f(inp["ffn2_w_gate"])[0], 16), f2u=tile_w(f(inp["ffn2_w_up"])[0], 16), f2d=tile_down(f(inp["ffn2_w_down"])[0]),
                  win_z=tile_w(w_in_full[:, 0:2048], 16), win_xbc=tile_w(w_in_full[:, 2048:5120], 16),
                  win_dt=tile_w(w_in_full[:, 5120:5152], 16, 32), win_glu=tile_w(w_in_full[:, 5152:7200], 16),
                  win_gate=tile_w(w_in_full[:, 7200:11296], 16),
                  w_sso=tile_w(f(inp["w_ssd_out"])[0], 16), w_cco=tile_w(f(inp["w_cc_out"])[0], 8), w_o=tile_w(f(inp["w_o"])[0], 16))
    maps = []
    for core in range(8):
        b, half = core // 2, core % 2
        m = dict(shared)
        m["xmain"] = xp[b, half * 1024:(half + 1) * 1024]
        m["xpre"] = xp[b, 0:1024] if half == 1 else np.zeros((1024, D), np.float32)
        m["flag"] = np.full((128, 1), float(half), np.float32)
        sl = slice(core * 16, (core + 1) * 16)
        m["xsam"] = xs[sl].reshape(128, D)
        m["st_ssm"] = f(inp["state_ssm"])[0, sl].reshape(16, 2048, 128)
        m["st_sconv"] = f(inp["state_ssm_conv"])[0, sl].reshape(48, 3072)
        m["st_cc"] = f(inp["state_cc_conv"])[0, sl].reshape(480, 1024)
        maps.append(m)
    return maps


def kernel(**inputs):
    nc = build()
    maps = _prep_inputs(inputs)
    res = run_bass_kernel_spmd(nc, maps, core_ids=list(range(8))).results
    yp = np.zeros((4, 2048, D), np.float32)
    for core in range(8):
        yp[core // 2, (core % 2) * 1024:(core % 2 + 1) * 1024] = res[core]["y_main"]
    ys = np.concatenate([res[c]["y_sam"].reshape(16, 8, D) for c in range(8)], axis=0)
    ssm_p = np.stack([res[2 * b + 1]["o_ssm_p"].reshape(32, 64, 128) for b in range(4)])[None]
    sconv_p = np.stack([res[2 * b + 1]["o_sconv_p"] for b in range(4)])[None]
    cc_p = np.stack([res[2 * b + 1]["o_cc_p"] for b in range(4)])[None]
    ssm_s = np.concatenate([res[c]["o_ssm_s"].reshape(16, 32, 64, 128) for c in range(8)], axis=0)[None]
    sconv_s = np.concatenate([res[c]["o_sconv_s"].reshape(16, 3, 3072) for c in range(8)], axis=0)[None]
    cc_s = np.concatenate([res[c]["o_cc_s"].reshape(16, 30, 1024) for c in range(8)], axis=0)[None]
    return (yp, ys, ssm_p, sconv_p, cc_p, ssm_s, sconv_s, cc_s)
```

```python
import numpy as np
import concourse.bass as bass
import concourse.mybir as mybir

F32 = mybir.dt.float32
BF16 = mybir.dt.bfloat16
F32R = mybir.dt.float32r
I32 = mybir.dt.int32
ALU = mybir.AluOpType
AF = mybir.ActivationFunctionType

ENGS = ("pe", "act", "dve", "pool", "sp")


class Tile:
    __slots__ = ("name", "ap", "last_w", "readers")

    def __init__(self, name, ap):
        self.name = name
        self.ap = ap
        self.last_w = None
        self.readers = []

    def __getitem__(self, idx):
        return self.ap[idx]


class Op:
    __slots__ = ("eng", "fn", "reads", "writes", "is_dma", "deps", "signals",
                 "sig", "dsem", "dval", "name")

    def __init__(self, eng, fn, reads, writes, is_dma, name):
        self.eng = eng
        self.fn = fn
        self.reads = reads
        self.writes = writes
        self.is_dma = is_dma
        self.deps = []
        self.signals = False
        self.sig = None
        self.dsem = None
        self.dval = None
        self.name = name


class Prog:
    def __init__(self, nc, n_dma_sems=24):
        self.nc = nc
        self.ops = []
        self.n_dma_sems = n_dma_sems

    def op(self, eng, fn, reads=(), writes=(), name=""):
        o = Op(eng, fn, list(reads), list(writes), False, name)
        self._track(o)
        return o

    def dma(self, queue, out_ap, in_ap, reads=(), writes=(), name=""):
        def fn(e, out_ap=out_ap, in_ap=in_ap):
            return e.dma_start(out=out_ap, in_=in_ap)
        o = Op(queue, fn, list(reads), list(writes), True, name)
        self._track(o)
        return o

    def _track(self, o):
        deps = []
        for t in o.reads:
            if t.last_w is not None:
                deps.append(t.last_w)
        for t in o.writes:
            if t.last_w is not None:
                deps.append(t.last_w)
            deps.extend(t.readers)
        seen = set()
        for d in deps:
            if id(d) in seen or d is o:
                continue
            seen.add(id(d))
            if (not d.is_dma) and (not o.is_dma) and d.eng == o.eng:
                if o.eng == "pe":
                    continue
            o.deps.append(d)
            d.signals = True
        for t in o.reads:
            t.readers.append(o)
        for t in o.writes:
            t.last_w = o
            t.readers = []
        self.ops.append(o)

    def emit(self, final_wait_eng="sp"):
        nc = self.nc
        for o in self.ops:
            if o.is_dma:
                o.signals = True
        cnt = {e: 0 for e in ENGS}
        for o in self.ops:
            if o.is_dma:
                continue
            if o.signals:
                cnt[o.eng] += 1
                o.sig = cnt[o.eng]
        import contextlib
        with contextlib.ExitStack() as st:
            esem = {e: st.enter_context(nc.semaphore("s_" + e)) for e in ENGS}
            dsems = {q: [st.enter_context(nc.semaphore(f"d_{q}{i}")) for i in range(self.n_dma_sems)]
                     for q in ("sp", "act", "pool")}
            duse = {q: [0] * self.n_dma_sems for q in dsems}
            dnext = {q: 0 for q in dsems}
            dma_ops = []
            for o in self.ops:
                if o.is_dma:
                    q = o.eng
                    i = dnext[q]
                    dnext[q] = (i + 1) % self.n_dma_sems
                    duse[q][i] += 1
                    o.dsem = (q, i)
                    o.dval = 16 * duse[q][i]
                    dma_ops.append(o)
            block = st.enter_context(nc.Block())
            by_eng = {e: [o for o in self.ops if o.eng == e] for e in ENGS}

            def replay(e, eng):
                known = {}

                def wait(key, sem, val):
                    if known.get(key, 0) >= val:
                        return
                    known[key] = val
                    e.wait_ge(sem, val)

                for o in by_eng[eng]:
                    for d in o.deps:
                        if d.is_dma:
                            wait(("d",) + d.dsem, dsems[d.dsem[0]][d.dsem[1]], d.dval)
                        else:
                            wait(("e", d.eng), esem[d.eng], d.sig)
                    if o.is_dma and o.dval > 16:
                        q, i = o.dsem
                        wait(("d", q, i), dsems[q][i], o.dval - 16)
                    inst = o.fn(e)
                    if o.is_dma:
                        q, i = o.dsem
                        inst.then_inc(dsems[q][i], 16)
                    elif o.signals:
                        inst.then_inc(esem[eng], 1)
                if eng == final_wait_eng:
                    last = {}
                    for o in dma_ops:
                        last[o.dsem] = max(last.get(o.dsem, 0), o.dval)
                    for (q, i), v in last.items():
                        wait(("d", q, i), dsems[q][i], v)
                    for en in ENGS:
                        if cnt[en] > 0 and en != eng:
                            wait(("e", en), esem[en], cnt[en])

            block.tensor(lambda e: replay(e, "pe"))
            block.scalar(lambda e: replay(e, "act"))
            block.vector(lambda e: replay(e, "dve"))
            block.gpsimd(lambda e: replay(e, "pool"))
            block.sync(lambda e: replay(e, "sp"))

import contextlib
from concourse.bass_utils import run_bass_kernel_spmd

D = 2048
DFF = 5632
NCH = 16
EPS = 1e-6
NEG = -30000.0


def build(dbg_stop=None):
    nc = bass.Bass("TRN2", target_bir_lowering=False)
    P = Prog(nc, n_dma_sems=24)
    st = contextlib.ExitStack()

    def din(name, shape):
        return nc.dram_tensor(name, list(shape), F32, kind="ExternalInput").ap()

    def dout(name, shape):
        return nc.dram_tensor(name, list(shape), F32, kind="ExternalOutput").ap()

    xpre = din("xpre", [1024, D]); xmain = din("xmain", [1024, D]); xsam = din("xsam", [128, D])
    flag_d = din("flag", [128, 1])
    st_ssm = din("st_ssm", [16, 2048, 128]); st_sconv = din("st_sconv", [48, 3072]); st_cc = din("st_cc", [480, 1024])
    vecD = din("vecD", [7, D])
    vec3072 = din("vec3072", [5, 3072])
    vec1024 = din("vec1024", [34, 1024])
    vecH = din("vecH", [3, 32])
    f1g = din("f1g", [44, 128, 16, 128]); f1u = din("f1u", [44, 128, 16, 128]); f1d = din("f1d", [32, 128, 22, 128])
    f2g = din("f2g", [44, 128, 16, 128]); f2u = din("f2u", [44, 128, 16, 128]); f2d = din("f2d", [32, 128, 22, 128])
    win_z = din("win_z", [16, 128, 16, 128]); win_xbc = din("win_xbc", [24, 128, 16, 128]); win_dt = din("win_dt", [1, 128, 16, 32])
    win_glu = din("win_glu", [16, 128, 16, 128]); win_gate = din("win_gate", [32, 128, 16, 128])
    w_sso = din("w_sso", [16, 128, 16, 128]); w_cco = din("w_cco", [16, 128, 8, 128]); w_o = din("w_o", [16, 128, 16, 128])
    y_main = dout("y_main", [1024, D]); y_sam = dout("y_sam", [128, D])
    o_ssm_p = dout("o_ssm_p", [2048, 128]); o_sconv_p = dout("o_sconv_p", [3, 3072]); o_cc_p = dout("o_cc_p", [30, 1024])
    o_ssm_s = dout("o_ssm_s", [16, 2048, 128]); o_sconv_s = dout("o_sconv_s", [48, 3072]); o_cc_s = dout("o_cc_s", [480, 1024])

    def sbt(name, shape, dt):
        return st.enter_context(nc.sbuf_tensor(name, list(shape), dt))

    def pst(name, shape, dt):
        return st.enter_context(nc.psum_tensor(name, list(shape), dt))

    TG = 576
    with st:
        st.enter_context(nc.allow_low_precision("bf16 matmul operands, fp32 accumulation"))
        st.enter_context(nc.allow_non_contiguous_dma("small strided constant loads"))
        R_t = sbt("R", [128, NCH, TG], F32); R = [Tile(f"R{c}", R_t[:, c, :]) for c in range(NCH)]
        Y_t = sbt("Y", [128, NCH, TG], F32); Y = [Tile(f"Y{c}", Y_t[:, c, :]) for c in range(NCH)]
        xn_t = sbt("xn", [128, NCH, TG], BF16); xn = [Tile(f"xn{c}", xn_t[:, c, :]) for c in range(NCH)]
        h_t = sbt("h", [128, 24, TG], BF16); H = [Tile(f"h{c}", h_t[:, c, :]) for c in range(24)]
        mg_t = sbt("mg", [128, NCH * TG], BF16)
        MG = [Tile(f"mg{c}", mg_t[:, c * TG:(c + 1) * TG]) for c in range(NCH)]
        xtok = Tile("xtok", mg_t[:, 0:2048]); xw = Tile("xw", mg_t[:, 2048:4096])
        btok = Tile("btok", mg_t[:, 4096:4608]); btm = Tile("btm", mg_t[:, 4608:5120])
        sTb = Tile("sTb", mg_t[:, 5120:7168]); hTb = Tile("hTb", mg_t[:, 7168:9216])
        WG = Tile("WG", mg_t[:, 5120:6144]); SCG = Tile("SCG", mg_t[:, 6144:7168])
        ALIAS = [xtok, xw, btok, btm, sTb, hTb, WG, SCG]
        NWB = 3
        wring = [Tile(f"w{i}", sbt(f"w{i}", [128, 22, 128], BF16)) for i in range(NWB)]
        wcount = [0]
        tmpA = [Tile(f"tA{i}", sbt(f"tA{i}", [128, TG], F32)) for i in range(2)]
        sqr = [Tile(f"sq{i}", sbt(f"sq{i}", [128, TG], BF16)) for i in range(2)]
        rstd = Tile("rstd", sbt("rstd", [128, TG], F32))
        aux = Tile("aux", sbt("aux", [128, TG], F32))
        dtF = aux
        xin = [Tile(f"xin{i}", sbt(f"xin{i}", [128, D], F32)) for i in range(2)]
        otok = xin
        cin = [Tile(f"cin{i}", sbt(f"cin{i}", [128, 2, 128], F32)) for i in range(2)]
        cin_n = [0]
        identf = Tile("identf", sbt("identf", [128, 128], F32))
        identb = Tile("identb", sbt("identb", [128, 128], BF16))
        ones_b = Tile("ones_b", sbt("ones_b", [128, 128], BF16))
        ones_f = Tile("ones_f", sbt("ones_f", [128, 128], F32))
        gD = Tile("gD", sbt("gD", [128, NCH, 7], F32))
        g3072 = Tile("g3072", sbt("g3072", [128, 24, 5], F32))
        g1024 = Tile("g1024", sbt("g1024", [128, 8, 34], F32))
        gH = Tile("gH", sbt("gH", [32, 4], F32))
        A_b = Tile("A_b", sbt("A_b", [128, 32], F32))
        Dcol = Tile("Dcol", sbt("Dcol", [128, 16], F32))
        flag = Tile("flag", sbt("flagt", [128, 1], F32))
        dummy = Tile("dummy", sbt("fdummy", [128, 2], F32))
        M1p = Tile("M1p", sbt("M1p", [128, 128], F32)); NEGp = Tile("NEGp", sbt("NEGp", [128, 128], F32))
        M1s = Tile("M1s", sbt("M1s", [128, 128], F32)); NEGs = Tile("NEGs", sbt("NEGs", [128, 128], F32))
        M2s = Tile("M2s", sbt("M2s", [128, 128], F32))
        seqind = Tile("seqind", sbt("seqind", [128, 8], F32))
        hist_x = Tile("hist_x", sbt("hist_x", [128, 24, 3], F32))
        hist_u = Tile("hist_u", sbt("hist_u", [128, 8, 30], F32))
        stg = [Tile(f"stg{i}", sbt(f"stg{i}", [128, 32 + 512], F32)) for i in range(2)]
        sstg = [Tile(f"sstg{i}", sbt(f"sstg{i}", [128, 8, 38], F32)) for i in range(2)]
        cacc = tmpA
        hT32 = Tile("hT32", sbt("hT32", [128, 2048], F32))
        cbt = Tile("cbt", sbt("cbt", [128, 4, 128], F32))
        dt_tm = Tile("dt_tm", sbt("dt_tm", [128, 32], F32)); dtA = Tile("dtA", sbt("dtA", [128, 32], F32))
        ncum = Tile("ncum", sbt("ncum", [128, 32], F32)); dedt = Tile("dedt", sbt("dedt", [128, 32], F32))
        ncumL = Tile("ncumL", sbt("ncumL", [128, 32], F32))
        decb = Tile("decb", sbt("decb", [128, 32], F32)); D_b = decb
        dnat = Tile("dnat", sbt("dnat", [128, 16], F32))
        wS = Tile("wS", sbt("wS", [128, 8, 64], BF16))
        ACC = [Tile(f"acc{i}", pst(f"acc{i}", [128, 2, 512], F32)) for i in range(3)]
        acc_i = [0]
        pm_t = [pst(f"pm{i}", [128, 4, 128], F32) for i in range(2)]
        PM = [Tile(f"pm{i}", pm_t[i]) for i in range(2)]
        pm_i = [0]

        reserved = [None]

        def next_acc():
            a = ACC[acc_i[0] % 3]; acc_i[0] += 1
            if a is reserved[0]:
                a = ACC[acc_i[0] % 3]; acc_i[0] += 1
            return a

        def next_pm():
            a = PM[pm_i[0] % 2]; pm_i[0] += 1; return a

        def V(ap, hw):
            return ap.rearrange("p (a b) -> p a b", b=hw)

        def halves(T):
            return (1, 512) if T == 512 else (2, T // 2)

        def wreq(Wt, idx, ncols, KC):
            t = wring[wcount[0] % NWB]; wcount[0] += 1
            P.dma("pool", t[:, 0:KC, 0:ncols], Wt[idx], writes=[t])
            return t

        def mm(wt, KC, ncols, rhs_tiles, T, extra_reads=()):
            return mm_multi([(wt, KC)], ncols, rhs_tiles, T)

        def mm_multi(wts, ncols, rhs_tiles, T):
            nh, hw = halves(T)
            acc = next_acc()
            KT = sum(k for _, k in wts)

            def fn(e):
                last = None
                kk = 0
                for wt, KC in wts:
                    for k in range(KC):
                        for a in range(nh):
                            last = e.matmul(acc[0:ncols, a, 0:hw], lhsT=wt[:, k, 0:ncols],
                                            rhs=rhs_tiles[kk][:, a * hw:(a + 1) * hw], start=(kk == 0), stop=(kk == KT - 1))
                        kk += 1
                return last
            P.op("pe", fn, reads=[w for w, _ in wts] + list(rhs_tiles[:KT]), writes=[acc])
            return acc

        def act(fn, reads, writes):
            return P.op("act", fn, reads=reads, writes=writes)

        def dve(fn, reads, writes):
            return P.op("dve", fn, reads=reads, writes=writes)

        def fence(tiles):
            dve(lambda e: e.memset(dummy[:, 0:1], 0.0), [], [dummy] + list(tiles))

        def sumsq_norm(S, T, gi, outs):
            nh, hw = halves(T)
            acc = next_acc()
            for c in range(NCH):
                sq = sqr[c % 2]
                act(lambda e, c=c, sq=sq: e.activation(out=sq[:, 0:T], in_=S[c][:, 0:T], func=AF.Square), [S[c]], [sq])

                def fn(e, c=c, sq=sq):
                    for a in range(nh):
                        last = e.matmul(acc[:, a, 0:hw], lhsT=ones_b[:, :], rhs=sq[:, a * hw:(a + 1) * hw],
                                        start=(c == 0), stop=(c == NCH - 1))
                    return last
                P.op("pe", fn, reads=[ones_b, sq], writes=[acc])
            act(lambda e: e.activation(out=V(rstd[:, 0:T], hw), in_=acc[:, 0:nh, 0:hw], func=AF.Ln, scale=1.0 / D, bias=EPS), [acc], [rstd])
            act(lambda e: e.activation(out=rstd[:, 0:T], in_=rstd[:, 0:T], func=AF.Exp, scale=-0.5), [rstd], [rstd])
            if outs is not None:
                for c in range(NCH):
                    dve(lambda e, c=c: e.scalar_tensor_tensor(out=outs[c][:, 0:T], in0=S[c][:, 0:T], scalar=gD[:, c, gi:gi + 1],
                                                              in1=rstd[:, 0:T], op0=ALU.mult, op1=ALU.mult), [S[c], gD, rstd], [outs[c]])

        def post_norm_residual(T, gi, factor):
            sumsq_norm(Y, T, gi, Y)
            for c in range(NCH):
                dve(lambda e, c=c: e.scalar_tensor_tensor(out=R[c][:, 0:T], in0=Y[c][:, 0:T], scalar=float(factor), in1=R[c][:, 0:T],
                                                          op0=ALU.mult, op1=ALU.add), [Y[c], R[c]], [R[c]])

        def ffn(T, wg, wu, wd, gpre, gpost):
            nh, hw = halves(T)
            sumsq_norm(R, T, gpre, xn)
            for half in range(2):
                for j in range(22):
                    jj = half * 22 + j
                    wG = wreq(wg, jj, 128, 16); G = mm(wG, 16, 128, xn, T)
                    wU = wreq(wu, jj, 128, 16); U = mm(wU, 16, 128, xn, T)
                    ta = tmpA[j % 2]
                    act(lambda e, G=G, ta=ta: e.activation(out=V(ta[:, 0:T], hw), in_=G[:, 0:nh, 0:hw], func=AF.Silu), [G], [ta])
                    dve(lambda e, U=U, ta=ta, j=j: e.tensor_tensor(out=V(H[j][:, 0:T], hw), in0=V(ta[:, 0:T], hw), in1=U[:, 0:nh, 0:hw], op=ALU.mult),
                        [U, ta], [H[j]])
                for c in range(NCH):
                    wD = wreq(wd, half * 16 + c, 128, 22); A = mm(wD, 22, 128, H, T)
                    if half == 0:
                        act(lambda e, A=A, c=c: e.activation(out=V(Y[c][:, 0:T], hw), in_=A[:, 0:nh, 0:hw], func=AF.Copy), [A], [Y[c]])
                    else:
                        dve(lambda e, A=A, c=c: e.tensor_tensor(out=V(Y[c][:, 0:T], hw), in0=V(Y[c][:, 0:T], hw), in1=A[:, 0:nh, 0:hw], op=ALU.add),
                            [A, Y[c]], [Y[c]])
            post_norm_residual(T, gpost, 0.5)

        def transpose_in(src_rows_ap, nt, col0, dst_tiles, dst_tensor, ring, ident, nchunks=NCH, c0=0):
            xi = ring[transpose_in.n % 2]; transpose_in.n += 1
            P.dma("sp", xi[0:nt, 0:nchunks * 128], src_rows_ap, writes=[xi])
            for q in range(0, nchunks, 4):
                pm = next_pm()
                nq = min(4, nchunks - q)

                def fn(e, q=q, pm=pm, nq=nq):
                    for i in range(nq):
                        last = e.transpose(out=pm[:, i, 0:nt], in_=xi[0:nt, (q + i) * 128:(q + i + 1) * 128], identity=ident[0:nt, 0:nt])
                    return last
                P.op("pe", fn, reads=[xi, ident], writes=[pm])
                act(lambda e, q=q, pm=pm, nq=nq: e.activation(out=dst_tensor[:, c0 + q:c0 + q + nq, col0:col0 + nt], in_=pm[:, 0:nq, 0:nt], func=AF.Copy),
                    [pm], dst_tiles[c0 + q:c0 + q + nq])
        transpose_in.n = 0

        def transpose_out(src_tensor, src_tiles, col0, nt, dst_rows_ap, nchunks=NCH, c0=0):
            ot = otok[transpose_in.n % 2]; transpose_in.n += 1
            for q in range(0, nchunks, 4):
                pm = next_pm()
                nq = min(4, nchunks - q)

                def fn(e, q=q, pm=pm, nq=nq):
                    for i in range(nq):
                        last = e.transpose(out=pm[0:nt, i, :], in_=src_tensor[:, c0 + q + i, col0:col0 + nt], identity=identf[:, :])
                    return last
                P.op("pe", fn, reads=list(src_tiles[c0 + q:c0 + q + nq]) + [identf], writes=[pm])
                act(lambda e, q=q, pm=pm, nq=nq: e.activation(out=ot[0:nt, q * 128:(q + nq) * 128].rearrange("p (a b) -> p a b", a=nq),
                                                            in_=pm[0:nt, 0:nq, :], func=AF.Copy), [pm], [ot])
            P.dma("sp", dst_rows_ap, ot[0:nt, 0:nchunks * 128], reads=[ot])

        pool_op = lambda fn, reads, writes: P.op("pool", fn, reads=reads, writes=writes)
        pool_op(lambda e: e.memset(ones_b[:, :], 1.0), [], [ones_b])
        pool_op(lambda e: e.memset(ones_f[:, :], 1.0), [], [ones_f])
        pool_op(lambda e: e.affine_select(out=identf[:, :], in_=ones_f[:, :], pattern=[[-1, 128]], compare_op=ALU.is_equal,
                                          fill=0.0, base=0, channel_multiplier=1), [ones_f], [identf])
        dve(lambda e: e.tensor_copy(out=identb[:, :], in_=identf[:, :]), [identf], [identb])
        pool_op(lambda e: e.affine_select(out=M1p[:, :], in_=ones_f[:, :], pattern=[[1, 128]], compare_op=ALU.is_ge,
                                          fill=0.0, base=0, channel_multiplier=-1), [ones_f], [M1p])
        dve(lambda e: e.tensor_scalar(out=NEGp[:, :], in0=M1p[:, :], scalar1=-1.0, scalar2=-NEG, op0=ALU.add, op1=ALU.mult), [M1p], [NEGp])
        pool_op(lambda e: e.affine_select(out=M2s[:, :].rearrange("p (j t) -> p j t", t=8), in_=ones_f[:, :].rearrange("p (j t) -> p j t", t=8),
                                          pattern=[[8, 16], [0, 8]], compare_op=ALU.is_ge, fill=0.0, base=7, channel_multiplier=-1),
                [ones_f], [M2s])
        pool_op(lambda e: e.affine_select(out=M2s[:, :].rearrange("p (j t) -> p j t", t=8), in_=M2s[:, :].rearrange("p (j t) -> p j t", t=8),
                                          pattern=[[-8, 16], [0, 8]], compare_op=ALU.is_ge, fill=0.0, base=0, channel_multiplier=1),
                [M2s], [M2s])
        dve(lambda e: e.tensor_tensor(out=M1s[:, :], in0=M2s[:, :], in1=M1p[:, :], op=ALU.mult), [M2s, M1p], [M1s])
        dve(lambda e: e.tensor_scalar(out=NEGs[:, :], in0=M1s[:, :], scalar1=-1.0, scalar2=-NEG, op0=ALU.add, op1=ALU.mult), [M1s], [NEGs])
        dve(lambda e: e.tensor_copy(out=seqind[:, :], in_=M2s[:, 0:64].rearrange("p (j t) -> p j t", t=8)[:, :, 0]), [M2s], [seqind])
        P.dma("sp", flag[:, :], flag_d[:, :], writes=[flag])
        P.dma("sp", gH[:, 0:3], vecH.rearrange("v h -> h v"), writes=[gH])
        P.dma("sp", A_b[:, :], vecH[1:2, :].broadcast_to([128, 32]), writes=[A_b])
        P.dma("sp", D_b[:, :], vecH[2:3, :].broadcast_to([128, 32]), writes=[D_b])
        act(lambda e: e.activation(out=A_b[:, :], in_=A_b[:, :], func=AF.Exp), [A_b], [A_b])
        dve(lambda e: e.tensor_scalar_mul(out=A_b[:, :], in0=A_b[:, :], scalar1=-1.0), [A_b], [A_b])
        dve(lambda e: e.tensor_copy(out=Dcol[0:64, :], in_=D_b[0:64, :].rearrange("p (c two) -> p c two", two=2)[:, :, 0]), [D_b], [Dcol])
        dve(lambda e: e.tensor_copy(out=Dcol[64:128, :], in_=D_b[64:128, :].rearrange("p (c two) -> p c two", two=2)[:, :, 1]), [D_b, Dcol], [Dcol])
        transpose_in(vecD[:, :], 7, 0, [gD] * NCH, gD.ap, xin, identf)
        transpose_in(vec3072[:, 0:2048], 5, 0, [g3072] * 24, g3072.ap, xin, identf, nchunks=16)
        transpose_in(vec3072[:, 2048:3072], 5, 0, [g3072] * 24, g3072.ap, xin, identf, nchunks=8, c0=16)
        transpose_in(vec1024[:, :], 34, 0, [g1024] * 8, g1024.ap, xin, identf, nchunks=8)
        pool_op(lambda e: e.memset(hist_x[:, :, :], 0.0), [], [hist_x])
        pool_op(lambda e: e.memset(hist_u[:, :, :], 0.0), [], [hist_u])
        pool_op(lambda e: e.memset(hT32[:, :], 0.0), [], [hT32])

        def ssd_chunk(col0, Q, sample, do_y):
            M1 = M1s if sample else M1p
            M2 = M2s if sample else ones_f
            cs = slice(col0, col0 + Q)
            pm = next_pm()
            P.op("pe", lambda e: e.transpose(out=pm[0:Q, 0, 0:32], in_=dtF[0:32, cs], identity=identf[0:32, 0:32]), [dtF, identf], [pm])
            act(lambda e: e.activation(out=dt_tm[0:Q, :], in_=pm[0:Q, 0, 0:32], func=AF.Copy), [pm], [dt_tm])
            dve(lambda e: e.tensor_tensor(out=dtA[0:Q, :], in0=dt_tm[0:Q, :], in1=A_b[0:Q, :], op=ALU.mult), [dt_tm, A_b], [dtA])
            pm2 = next_pm()

            def fn2(e):
                e.matmul(pm2[0:Q, 0, 0:32], lhsT=M1[0:Q, 0:Q], rhs=dtA[0:Q, :], start=True, stop=True)
                return e.matmul(pm2[0:Q, 1, 0:32], lhsT=M2[0:Q, 0:Q], rhs=dtA[0:Q, :], start=True, stop=True)
            P.op("pe", fn2, [M1, M2, dtA], [pm2])
            dve(lambda e: e.tensor_scalar_mul(out=ncum[0:Q, :], in0=pm2[0:Q, 0, 0:32], scalar1=-1.0), [pm2], [ncum])
            act(lambda e: e.activation(out=ncumL[0:Q, :], in_=dt_tm[0:Q, :], func=AF.Ln), [dt_tm], [ncumL])
            dve(lambda e: e.tensor_tensor(out=ncumL[0:Q, :], in0=ncumL[0:Q, :], in1=ncum[0:Q, :], op=ALU.add), [ncumL, ncum], [ncumL])
            dve(lambda e: e.tensor_tensor(out=dedt[0:Q, :], in0=pm2[0:Q, 1, 0:32], in1=ncum[0:Q, :], op=ALU.add), [pm2, ncum], [dedt])
            act(lambda e: e.activation(out=dedt[0:Q, :], in_=dedt[0:Q, :], func=AF.Exp), [dedt], [dedt])
            dve(lambda e: e.tensor_tensor(out=dedt[0:Q, :], in0=dedt[0:Q, :], in1=dt_tm[0:Q, :], op=ALU.mult), [dedt, dt_tm], [dedt])
            for q in range(0, 16, 4):
                pmx = next_pm()
                pmx_b = pmx.ap.bitcast(BF16)

                def fnx(e, q=q, pmx_b=pmx_b):
                    for i in range(4):
                        last = e.transpose(out=pmx_b[0:Q, i, 0:128], in_=H[q + i][:, cs], identity=identb[:, :])
                    return last
                P.op("pe", fnx, reads=H[q:q + 4] + [identb], writes=[pmx])
                act(lambda e, q=q, pmx_b=pmx_b: e.activation(out=xtok[0:Q, q * 128:(q + 4) * 128].rearrange("p (a b) -> p a b", a=4),
                                                             in_=pmx_b[0:Q, 0:4, 0:128], func=AF.Copy), [pmx], [xtok])
            dve(lambda e: e.tensor_tensor(out=xw[0:Q, :].rearrange("p (h d) -> p h d", d=64), in0=xtok[0:Q, :].rearrange("p (h d) -> p h d", d=64),
                                          in1=dedt[0:Q, :].unsqueeze(2).broadcast_to([Q, 32, 64]), op=ALU.mult), [xtok, dedt], [xw])
            pmb = next_pm()
            pmb_b = pmb.ap.bitcast(BF16)

            def fnb(e):
                for i in range(4):
                    last = e.transpose(out=pmb_b[0:Q, i, 0:128], in_=H[16 + i][:, cs], identity=identb[:, :])
                return last
            P.op("pe", fnb, reads=H[16:20] + [identb], writes=[pmb])
            act(lambda e: e.activation(out=btok[0:Q, :].rearrange("p (a b) -> p a b", a=4), in_=pmb_b[0:Q, 0:4, 0:128], func=AF.Copy), [pmb], [btok])
            if do_y:
                pmc = next_pm()

                def fnc(e):
                    for g in range(4):
                        last = e.matmul(pmc[0:Q, g, 0:Q], lhsT=H[16 + g][:, cs], rhs=H[20 + g][:, cs], start=True, stop=True)
                    return last
                P.op("pe", fnc, reads=H[16:24], writes=[pmc])
                act(lambda e: e.activation(out=cbt[0:Q, :, 0:Q], in_=pmc[0:Q, :, 0:Q], func=AF.Copy), [pmc], [cbt])

        def ssd_heads_prompt(col0):
            Q = 128
            cs = slice(col0, col0 + Q)
            wgv = WG.ap.rearrange("p (j t) -> p j t", t=128)
            scv = SCG.ap.rearrange("p (j t) -> p j t", t=128)
            def fcum_p(g):
                A1_ = next_acc()
                a1v_ = A1_.ap.rearrange("p a (j t) -> p (a j) t", t=128)

                def fcum(e):
                    for j in range(8):
                        h = 8 * g + j
                        last = e.matmul(a1v_[:, j, :], lhsT=dtA[0:Q, h:h + 1].broadcast_to([Q, 128]), rhs=M1p[0:Q, 0:Q], start=True, stop=True)
                    return last
                P.op("pe", fcum, [dtA, M1p], [A1_])
                return A1_, a1v_
            nxt = fcum_p(0)
            for g in range(4):
                A1, a1v = nxt
                A2 = next_acc()
                a2v = A2.ap.rearrange("p a (j t) -> p (a j) t", t=128)
                hs = slice(8 * g, 8 * g + 8)
                if g + 1 < 4:
                    nxt = fcum_p(g + 1)
                dve(lambda e, a1v=a1v, a2v=a2v, hs=hs: e.tensor_tensor(out=a2v, in0=a1v, in1=ncumL[0:Q, hs].unsqueeze(2).broadcast_to([Q, 8, 128]), op=ALU.add),
                    [A1, ncumL], [A2])
                dve(lambda e, a2v=a2v: e.tensor_tensor(out=a2v, in0=a2v, in1=NEGp[:, :].unsqueeze(1).broadcast_to([128, 8, 128]), op=ALU.add), [A2, NEGp], [A2])
                act(lambda e, a2v=a2v: e.activation(out=a2v, in_=a2v, func=AF.Exp), [A2], [A2])
                act(lambda e, a1v=a1v: e.activation(out=a1v, in_=a1v, func=AF.Exp), [A1], [A1])
                dve(lambda e, a2v=a2v, g=g: e.tensor_tensor(out=wgv, in0=a2v, in1=cbt[0:Q, g, 0:Q].unsqueeze(1).broadcast_to([Q, 8, Q]), op=ALU.mult),
                    [A2, cbt], [WG])
                dve(lambda e, a1v=a1v, g=g: e.tensor_tensor(out=scv, in0=a1v, in1=H[20 + g][:, cs].unsqueeze(1).broadcast_to([128, 8, Q]), op=ALU.mult),
                    [A1, H[20 + g]], [SCG])
                for jp in range(4):
                    pr = 4 * g + jp
                    pmy = next_pm()

                    def fny(e, pmy=pmy, jp=jp, g=g):
                        for hh in range(2):
                            j = 2 * jp + hh
                            h = 8 * g + j
                            e.matmul(pmy[64 * hh:64 * hh + 64, 3, 0:Q], lhsT=xtok[0:Q, h * 64:(h + 1) * 64], rhs=wgv[0:Q, j, :], start=True, stop=False)
                            last = e.matmul(pmy[64 * hh:64 * hh + 64, 3, 0:Q], lhsT=hTb[:, h * 64:(h + 1) * 64], rhs=scv[:, j, :], start=False, stop=True)
                        return last
                    P.op("pe", fny, [xtok, WG, SCG, hTb], [pmy])
                    dve(lambda e, pr=pr, pmy=pmy: e.scalar_tensor_tensor(out=Y[pr][:, cs], in0=H[pr][:, cs], scalar=Dcol[:, pr:pr + 1],
                                                                         in1=pmy[:, 3, 0:Q], op0=ALU.mult, op1=ALU.add), [H[pr], Dcol, pmy], [Y[pr]])

        def state_update(Q, bt_tile, s32, sb16, jcol=None):
            pmd = next_pm()
            if jcol is None:
                P.op("pe", lambda e: e.matmul(pmd[:, 0, 0:32], lhsT=ones_f[0:Q, :], rhs=dtA[0:Q, :], start=True, stop=True), [ones_f, dtA], [pmd])
            else:
                P.op("pe", lambda e: e.matmul(pmd[:, 0, 0:32], lhsT=seqind[0:Q, jcol:jcol + 1].broadcast_to([Q, 128]), rhs=dtA[0:Q, :], start=True, stop=True),
                     [seqind, dtA], [pmd])
            act(lambda e: e.activation(out=decb[:, :], in_=pmd[:, 0, 0:32], func=AF.Exp), [pmd], [decb])
            dve(lambda e: e.tensor_tensor(out=s32[:, :].rearrange("p (h d) -> p h d", d=64), in0=s32[:, :].rearrange("p (h d) -> p h d", d=64),
                                          in1=decb[:, :].unsqueeze(2).broadcast_to([128, 32, 64]), op=ALU.mult), [s32, decb], [s32])
            for g in range(4):
                a = next_acc()
                P.op("pe", lambda e, g=g, a=a: e.matmul(a[:, 0, :], lhsT=bt_tile[0:Q, g * 128:(g + 1) * 128], rhs=xw[0:Q, g * 512:(g + 1) * 512],
                                                        start=True, stop=True), [bt_tile, xw], [a])
                dve(lambda e, g=g, a=a: e.tensor_tensor(out=s32[:, g * 512:(g + 1) * 512], in0=s32[:, g * 512:(g + 1) * 512], in1=a[:, 0, :], op=ALU.add),
                    [a, s32], [s32])
            if sb16 is not None:
                act(lambda e: e.activation(out=sb16[:, :], in_=s32[:, :], func=AF.Copy), [s32], [sb16])

        def store_state_nat(s32, dst_ap):
            sn_t = xin[transpose_in.n % 2]; transpose_in.n += 1
            snv = sn_t.ap.rearrange("p (a n) -> p a n", a=16)
            for q in range(0, 16, 4):
                pm = next_pm()

                def fn(e, q=q, pm=pm):
                    for i in range(4):
                        last = e.transpose(out=pm[:, i, :], in_=s32[:, (q + i) * 128:(q + i + 1) * 128], identity=identf[:, :])
                    return last
                P.op("pe", fn, [s32, identf], [pm])
                act(lambda e, q=q, pm=pm: e.activation(out=snv[:, q:q + 4, :], in_=pm[:, :, :], func=AF.Copy), [pm], [sn_t])
            P.dma("sp", dst_ap.rearrange("(a p) n -> p a n", p=128), snv[:, :, :], reads=[sn_t])

        def sample_ssd(gidx):
            Q = 64
            cs = slice(512, 576)
            ssd_chunk(512, Q, True, True)
            ys = next_acc()
            reserved[0] = ys
            ysv = ys.ap.rearrange("p a (c t) -> p (a c) t", t=64)
            scall = hTb
            def fclr(e):
                e.matmul(ys[:, 0, :], lhsT=NEGp[0:1, :], rhs=R_t[0:1, 0, 0:512], start=True, stop=False, skip_group_check=True)
                return e.matmul(ys[:, 1, :], lhsT=NEGp[0:1, :], rhs=R_t[0:1, 0, 0:512], start=True, stop=False, skip_group_check=True)
            P.op("pe", fclr, [NEGp, R[0]], [ys])
            scv = scall.ap.rearrange("p (h t) -> p h t", t=64)
            def fcum_s(g):
                A1_ = next_acc()
                a1v_ = A1_[:, 0, :].rearrange("p (j t) -> p j t", t=64)

                def fcum(e):
                    for j in range(8):
                        h = 8 * g + j
                        last = e.matmul(a1v_[:, j, :], lhsT=dtA[0:Q, h:h + 1].broadcast_to([Q, 128]), rhs=M1s[0:Q, 0:Q], start=True, stop=True)
                    return last
                P.op("pe", fcum, [dtA, M1s], [A1_])
                return A1_, a1v_
            for g in range(4):
                A1, a1v = fcum_s(g)
                A2 = next_acc()
                a2v = A2[:, 0, :].rearrange("p (j t) -> p j t", t=64)
                hs = slice(8 * g, 8 * g + 8)
                dve(lambda e, a1v=a1v, a2v=a2v, hs=hs: e.tensor_tensor(out=a2v[0:Q], in0=a1v[0:Q], in1=ncumL[0:Q, hs].unsqueeze(2).broadcast_to([Q, 8, Q]), op=ALU.add),
                    [A1, ncumL], [A2])
                dve(lambda e, a2v=a2v: e.tensor_tensor(out=a2v[0:Q], in0=a2v[0:Q], in1=NEGs[0:Q, 0:Q].unsqueeze(1).broadcast_to([Q, 8, Q]), op=ALU.add), [A2, NEGs], [A2])
                act(lambda e, a2v=a2v: e.activation(out=a2v[0:Q], in_=a2v[0:Q], func=AF.Exp), [A2], [A2])
                act(lambda e, a1v=a1v: e.activation(out=a1v, in_=a1v, func=AF.Exp), [A1], [A1])
                dve(lambda e, a2v=a2v, g=g: e.tensor_tensor(out=wS[0:Q, :, :], in0=a2v[0:Q], in1=cbt[0:Q, g, 0:Q].unsqueeze(1).broadcast_to([Q, 8, Q]), op=ALU.mult),
                    [A2, cbt], [wS])
                dve(lambda e, a1v=a1v, g=g: e.tensor_tensor(out=scv[:, 8 * g:8 * g + 8, :], in0=a1v, in1=H[20 + g][:, cs].unsqueeze(1).broadcast_to([128, 8, Q]), op=ALU.mult),
                    [A1, H[20 + g]], [scall])

                def fyd(e, g=g):
                    for j in range(8):
                        h = 8 * g + j
                        pr, hh = h // 2, h % 2
                        last = e.matmul(ysv[64 * hh:64 * hh + 64, pr, 0:Q], lhsT=xtok[0:Q, h * 64:(h + 1) * 64], rhs=wS[0:Q, j, :],
                                        start=False, stop=False, skip_group_check=True)
                    return last
                P.op("pe", fyd, [xtok, wS], [ys])
            for j in range(8):
                seq = gidx * 8 + j
                sn_t = xin[transpose_in.n % 2]; transpose_in.n += 1
                so_t = xin[transpose_in.n % 2]; transpose_in.n += 1
                snv = sn_t.ap.rearrange("p (a n) -> p a n", a=16)
                sov = so_t.ap.rearrange("p (a n) -> p a n", a=16)
                P.dma("sp", snv[:, :, :], st_ssm[seq].rearrange("(a p) n -> p a n", p=128), writes=[sn_t])
                for q in range(0, 16, 4):
                    pm = next_pm()

                    def fn(e, q=q, pm=pm, snv=snv):
                        for i in range(4):
                            last = e.transpose(out=pm[:, i, :], in_=snv[:, q + i, :], identity=identf[:, :])
                        return last
                    P.op("pe", fn, [sn_t, identf], [pm])
                    act(lambda e, q=q, pm=pm: e.activation(out=sTb[:, q * 128:(q + 4) * 128].rearrange("p (a b) -> p a b", a=4), in_=pm[:, :, :], func=AF.Copy),
                        [pm], [sTb])

                def fno(e, j=j):
                    for h in range(32):
                        pr, hh = h // 2, h % 2
                        last = e.matmul(ysv[64 * hh:64 * hh + 64, pr, 8 * j:8 * j + 8], lhsT=sTb[:, h * 64:(h + 1) * 64],
                                        rhs=scall[:, h * 64 + 8 * j:h * 64 + 8 * j + 8], start=False, stop=(j == 7), skip_group_check=True)
                    return last
                P.op("pe", fno, [sTb, scall], [ys])
                dve(lambda e, j=j: e.tensor_scalar(out=btm[0:Q, :], in0=btok[0:Q, :], scalar1=seqind[0:Q, j:j + 1], scalar2=None, op0=ALU.mult),
                    [btok, seqind], [btm])
                pmd = next_pm()
                P.op("pe", lambda e, j=j, pmd=pmd: e.matmul(pmd[:, 0, 0:32], lhsT=seqind[0:Q, j:j + 1].broadcast_to([Q, 128]), rhs=dtA[0:Q, :], start=True, stop=True),
                     [seqind, dtA], [pmd])
                act(lambda e, pmd=pmd: e.activation(out=decb[:, :], in_=pmd[:, 0, 0:32], func=AF.Exp), [pmd], [decb])
                dve(lambda e: e.tensor_copy(out=dnat[0:64, :], in_=decb[0:64, :].rearrange("p (c two) -> p c two", two=2)[:, :, 0]), [decb], [dnat])
                dve(lambda e: e.tensor_copy(out=dnat[64:128, :], in_=decb[64:128, :].rearrange("p (c two) -> p c two", two=2)[:, :, 1]), [decb, dnat], [dnat])
                for q in range(4):
                    pm = next_pm()

                    def fup(e, q=q, pm=pm):
                        for i in range(4):
                            t = 4 * q + i
                            last = e.matmul(pm[:, i, :], lhsT=xw[0:Q, t * 128:(t + 1) * 128], rhs=btm[0:Q, q * 128:(q + 1) * 128], start=True, stop=True)
                        return last
                    P.op("pe", fup, [xw, btm], [pm])
                    for i in range(4):
                        t = 4 * q + i
                        dve(lambda e, t=t, i=i, pm=pm, snv=snv, sov=sov: e.scalar_tensor_tensor(out=sov[:, t, :], in0=snv[:, t, :], scalar=dnat[:, t:t + 1],
                                                                                              in1=pm[:, i, :], op0=ALU.mult, op1=ALU.add),
                            [sn_t, dnat, pm], [so_t])
                P.dma("sp", o_ssm_s[seq].rearrange("(a p) n -> p a n", p=128), sov[:, :, :], reads=[so_t])
            reserved[0] = None
            for pr in range(16):
                dve(lambda e, pr=pr: e.scalar_tensor_tensor(out=Y[pr][:, cs], in0=H[pr][:, cs], scalar=Dcol[:, pr:pr + 1],
                                                            in1=ysv[:, pr, :], op0=ALU.mult, op1=ALU.add), [H[pr], Dcol, ys], [Y[pr]])

        groups = [("pre", 0), ("pre", 1), ("main", 0), ("main", 1)]
        def group_body(gi_, kind, gidx):
            main = kind == "main"
            T = 576 if main else 512
            nh, hw = halves(T)
            xsrc = xmain if main else xpre
            for tt in range(4):
                transpose_in(xsrc[gidx * 512 + tt * 128: gidx * 512 + (tt + 1) * 128, :], 128, tt * 128, R, R_t, xin, identf)
            if main:
                transpose_in(xsam[gidx * 64:(gidx + 1) * 64, :], 64, 512, R, R_t, xin, identf)
            ffn(T, f1g, f1u, f1d, 0, 1)
            if dbg_stop == "ffn1":
                for tt in range(4):
                    transpose_out(R_t, R, tt * 128, 128, y_main[tt * 128:(tt + 1) * 128, :])
                return True
            sumsq_norm(R, T, 2, xn)
            def xbc_mm(c):
                wt = wreq(win_xbc, c, 128, 16)
                return mm(wt, 16, 128, xn, T)
            def xbc_evac(c, A):
                sg = stg[c % 2]; ss = sstg[c % 2]
                dve(lambda e: e.tensor_copy(out=sg[:, 29:32], in_=hist_x[:, c, :]), [hist_x], [sg])
                act(lambda e: e.activation(out=sg[:, 32:32 + hw], in_=A[:, 0, 0:hw], func=AF.Copy), [A], [sg])
                if nh == 2:
                    act(lambda e: e.activation(out=sg[:, 32 + hw:32 + 512], in_=A[:, 1, 0:512 - hw], func=AF.Copy), [A], [sg])
                dve(lambda e: e.tensor_copy(out=hist_x[:, c, :], in_=sg[:, 541:544]), [sg], [hist_x])

            def xbc_conv(c, A):
                sg = stg[c % 2]; ss = sstg[c % 2]; ca = cacc[c % 2]
                if main:
                    ci = cin[cin_n[0] % 2]; cin_n[0] += 1
                    P.dma("sp", ci[0:24, 0, :], st_sconv[gidx * 24:(gidx + 1) * 24, c * 128:(c + 1) * 128], writes=[ci])
                    pmq = next_pm()
                    P.op("pe", lambda e: e.transpose(out=pmq[:, 0, 0:24], in_=ci[0:24, 0, :], identity=identf[0:24, 0:24]), [ci, identf], [pmq])
                    act(lambda e: e.activation(out=ss[:, :, 0:3], in_=pmq[:, 0, 0:24].rearrange("p (j t) -> p j t", t=3), func=AF.Copy), [pmq], [ss])
                    act(lambda e: e.activation(out=ss[:, :, 3:11], in_=A[:, 1, 512 - hw:hw].rearrange("p (j t) -> p j t", t=8), func=AF.Copy), [A], [ss])
                dve(lambda e: e.tensor_scalar(out=ca[:, 0:512], in0=sg[:, 29:29 + 512], scalar1=g3072[:, c, 0:1], scalar2=None, op0=ALU.mult), [sg, g3072], [ca])
                for k in range(1, 4):
                    dve(lambda e, k=k: e.scalar_tensor_tensor(out=ca[:, 0:512], in0=sg[:, 29 + k:29 + k + 512], scalar=g3072[:, c, k:k + 1],
                                                              in1=ca[:, 0:512], op0=ALU.mult, op1=ALU.add), [sg, g3072, ca], [ca])
                if main:
                    pmo = next_pm()
                    ct = cin[cin_n[0] % 2]; cin_n[0] += 1
                    ctv = ct.ap.rearrange("p a b -> p (a b)")
                    dve(lambda e: e.tensor_copy(out=ctv[:, 0:24].rearrange("p (j t) -> p j t", t=3), in_=ss[:, :, 8:11]), [ss], [ct])
                    P.op("pe", lambda e: e.transpose(out=pmo[0:24, 0, :], in_=ctv[:, 0:24], identity=identf[:, :]), [ct, identf], [pmo])
                    co = cin[cin_n[0] % 2]; cin_n[0] += 1
                    act(lambda e: e.activation(out=co[0:24, 0, :], in_=pmo[0:24, 0, :], func=AF.Copy), [pmo], [co])
                    P.dma("sp", o_sconv_s[gidx * 24:(gidx + 1) * 24, c * 128:(c + 1) * 128], co[0:24, 0, :], reads=[co])
                    cav = ca[:, 512:576].rearrange("p (j t) -> p j t", t=8)
                    dve(lambda e: e.tensor_scalar(out=cav, in0=ss[:, :, 0:8], scalar1=g3072[:, c, 0:1], scalar2=None, op0=ALU.mult), [ss, g3072], [ca])
                    for k in range(1, 4):
                        dve(lambda e, k=k: e.scalar_tensor_tensor(out=cav, in0=ss[:, :, k:k + 8], scalar=g3072[:, c, k:k + 1],
                                                                  in1=cav, op0=ALU.mult, op1=ALU.add), [ss, g3072, ca], [ca])
                act(lambda e: e.activation(out=H[c][:, 0:T], in_=ca[:, 0:T], func=AF.Silu, bias=g3072[:, c, 4:5]), [ca, g3072], [H[c]])

            accs = {0: xbc_mm(0), 1: xbc_mm(1)}
            xbc_evac(0, accs[0])
            for c in range(24):
                if c + 2 < 24:
                    accs[c + 2] = xbc_mm(c + 2)
                if c + 1 < 24:
                    xbc_evac(c + 1, accs[c + 1])
                xbc_conv(c, accs[c])
            if dbg_stop == "xbc2" and main:
                for c in range(NCH):
                    dve(lambda e, c=c: e.tensor_copy(out=Y[c][:, 0:T], in_=H[c][:, 0:T]), [H[c]], [Y[c]])
                for tt in range(4):
                    transpose_out(Y_t, Y, tt * 128, 128, y_main[tt * 128:(tt + 1) * 128, :])
                transpose_out(Y_t, Y, 512, 64, y_sam[0:64, :])
                return True
            wt = wreq(win_dt, 0, 32, 16); A = mm(wt, 16, 32, xn, T)
            act(lambda e, A=A: e.activation(out=V(dtF[0:32, 0:T], hw), in_=A[0:32, 0:nh, 0:hw], func=AF.Exp, bias=gH[0:32, 0:1]), [A, gH], [dtF])
            act(lambda e: e.activation(out=dtF[0:32, 0:T], in_=dtF[0:32, 0:T], func=AF.Ln, bias=1.0), [dtF], [dtF])
            fence(MG + ALIAS)
            act(lambda e: e.activation(out=hTb[:, :], in_=hT32[:, :], func=AF.Copy), [hT32], [hTb])
            if not main and gidx == 1:
                pass
            for ch in range(4):
                ssd_chunk(ch * 128, 128, False, main)
                if main:
                    ssd_heads_prompt(ch * 128)
                state_update(128, btok, hT32, hTb)
            if dbg_stop == "ssd" and gi_ == 0:
                store_state_nat(hT32, o_ssm_p)
                return True
            if not main:
                if gidx == 1:
                    dve(lambda e: e.tensor_scalar(out=hT32[:, :], in0=hT32[:, :], scalar1=flag[:, 0:1], scalar2=None, op0=ALU.mult), [hT32, flag], [hT32])
                    for c in range(8):
                        wa = wreq(win_glu, c, 128, 16); wb = wreq(win_glu, 8 + c, 128, 16)
                        a = next_acc()

                        def fng(e, wa=wa, wb=wb, a=a):
                            for k in range(16):
                                e.matmul(a[:, 0, 0:32], lhsT=wa[:, k, :], rhs=xn[k][:, 480:512], start=(k == 0), stop=(k == 15))
                            for k in range(16):
                                last = e.matmul(a[:, 1, 0:32], lhsT=wb[:, k, :], rhs=xn[k][:, 480:512], start=(k == 0), stop=(k == 15))
                            return last
                        P.op("pe", fng, [wa, wb] + xn, [a])
                        ta = tmpA[0]
                        act(lambda e, a=a, ta=ta: e.activation(out=ta[:, 0:32], in_=a[:, 1, 0:32], func=AF.Sigmoid), [a], [ta])
                        dve(lambda e, a=a, ta=ta, c=c: e.tensor_tensor(out=hist_u[:, c, :], in0=ta[:, 2:32], in1=a[:, 0, 2:32], op=ALU.mult), [a, ta], [hist_u])
                    if dbg_stop == "pre":
                        store_state_nat(hT32, o_ssm_p)
                        transpose_out(hist_x.ap, [hist_x] * 24, 0, 3, o_sconv_p[:, 0:2048], nchunks=16)
                        transpose_out(hist_x.ap, [hist_x] * 24, 0, 3, o_sconv_p[:, 2048:3072], nchunks=8, c0=16)
                        transpose_out(hist_u.ap, [hist_u] * 8, 0, 30, o_cc_p[:, :], nchunks=8)
                        return True
                return False
            if gidx == 1:
                store_state_nat(hT32, o_ssm_p)
                transpose_out(hist_x.ap, [hist_x] * 24, 0, 3, o_sconv_p[:, 0:2048], nchunks=16)
                transpose_out(hist_x.ap, [hist_x] * 24, 0, 3, o_sconv_p[:, 2048:3072], nchunks=8, c0=16)
            fence([sTb, WG, SCG])
            sample_ssd(gidx)
            if dbg_stop == "y":
                for tt in range(4):
                    transpose_out(Y_t, Y, tt * 128, 128, y_main[tt * 128:(tt + 1) * 128, :])
                transpose_out(Y_t, Y, 512, 64, y_sam[0:64, :])
                return True
            for c in range(NCH):
                wt = wreq(win_z, c, 128, 16); A = mm(wt, 16, 128, xn, T)
                ta = tmpA[c % 2]
                act(lambda e, A=A, ta=ta: e.activation(out=V(ta[:, 0:T], hw), in_=A[:, 0:nh, 0:hw], func=AF.Silu), [A], [ta])
                dve(lambda e, ta=ta, c=c: e.tensor_tensor(out=Y[c][:, 0:T], in0=Y[c][:, 0:T], in1=ta[:, 0:T], op=ALU.mult), [ta, Y[c]], [Y[c]])
            sumsq_norm(Y, T, 6, H[0:16])
            if dbg_stop == "yn":
                dve(lambda e: e.memset(dummy[:, 1:2], 0.0), [], [dummy])
                for c in range(NCH):
                    dve(lambda e, c=c: e.tensor_copy(out=Y[c][:, 0:T], in_=H[c][:, 0:T]), [H[c]], [Y[c]])
                for tt in range(4):
                    transpose_out(Y_t, Y, tt * 128, 128, y_main[tt * 128:(tt + 1) * 128, :])
                transpose_out(Y_t, Y, 512, 64, y_sam[0:64, :])
                return True
            def glu_mm(c):
                wa = wreq(win_glu, c, 128, 16); GA_ = mm(wa, 16, 128, xn, T)
                wb = wreq(win_glu, 8 + c, 128, 16); GB_ = mm(wb, 16, 128, xn, T)
                return GA_, GB_
            G_next = glu_mm(0)
            for c in range(8):
                GA, GB = G_next
                ta = tmpA[c % 2]; sg = stg[c % 2]; ss = sstg[0]
                act(lambda e, GB=GB, ta=ta: e.activation(out=V(ta[:, 0:T], hw), in_=GB[:, 0:nh, 0:hw], func=AF.Sigmoid), [GB], [ta])
                dve(lambda e, sg=sg, c=c: e.tensor_copy(out=sg[:, 2:32], in_=hist_u[:, c, :]), [hist_u], [sg])
                dve(lambda e, sg=sg, GA=GA, ta=ta: e.tensor_tensor(out=sg[:, 32:32 + hw], in0=GA[:, 0, 0:hw], in1=ta[:, 0:hw], op=ALU.mult), [GA, ta], [sg])
                dve(lambda e, sg=sg, GA=GA, ta=ta: e.tensor_tensor(out=sg[:, 32 + hw:32 + 512], in0=GA[:, 1, 0:512 - hw], in1=ta[:, hw:512], op=ALU.mult), [GA, ta], [sg])
                dve(lambda e, sg=sg, c=c: e.tensor_copy(out=hist_u[:, c, :], in_=sg[:, 514:544]), [sg], [hist_u])
                dve(lambda e, ss=ss, GA=GA, ta=ta: e.tensor_tensor(out=ss[:, :, 30:38], in0=GA[:, 1, 512 - hw:hw].rearrange("p (j t) -> p j t", t=8),
                                                                 in1=ta[:, 512:576].rearrange("p (j t) -> p j t", t=8), op=ALU.mult), [GA, ta], [ss])
                if c + 1 < 8:
                    G_next = glu_mm(c + 1)
                ci = cin[cin_n[0] % 2]; cin_n[0] += 1
                P.dma("sp", ci[0:120, :, :], st_cc[gidx * 240:(gidx + 1) * 240, c * 128:(c + 1) * 128].rearrange("(q r) n -> r q n", q=2), writes=[ci])
                pmq = next_pm()

                def fnq(e, ci=ci, pmq=pmq):
                    e.transpose(out=pmq[:, 0, 0:120], in_=ci[0:120, 0, :], identity=identf[0:120, 0:120])
                    return e.transpose(out=pmq[:, 1, 0:120], in_=ci[0:120, 1, :], identity=identf[0:120, 0:120])
                P.op("pe", fnq, [ci, identf], [pmq])
                for q in range(2):
                    act(lambda e, ss=ss, pmq=pmq, q=q: e.activation(out=ss[:, 4 * q:4 * q + 4, 0:30], in_=pmq[:, q, 0:120].rearrange("p (j t) -> p j t", t=30), func=AF.Copy), [pmq], [ss])
                pmo = next_pm()

                ct = cin[cin_n[0] % 2]; cin_n[0] += 1
                ctv = ct.ap.rearrange("p a b -> p (a b)")
                dve(lambda e, ss=ss, ctv=ctv: e.tensor_copy(out=ctv[:, 0:240].rearrange("p (j t) -> p j t", t=30), in_=ss[:, :, 8:38]), [ss], [ct])

                def fno2(e, ctv=ctv, pmo=pmo):
                    e.transpose(out=pmo[0:120, 0, :], in_=ctv[:, 0:120], identity=identf[:, :])
                    return e.transpose(out=pmo[0:120, 1, :], in_=ctv[:, 120:240], identity=identf[:, :])
                P.op("pe", fno2, [ct, identf], [pmo])
                co = cin[cin_n[0] % 2]; cin_n[0] += 1
                act(lambda e, co=co, pmo=pmo: e.activation(out=co[0:120, :, :], in_=pmo[0:120, 0:2, :], func=AF.Copy), [pmo], [co])
                P.dma("sp", o_cc_s[gidx * 240:(gidx + 1) * 240, c * 128:(c + 1) * 128].rearrange("(q r) n -> r q n", q=2), co[0:120, :, :], reads=[co])
                yv = Y[c][:, 512:576].rearrange("p (j t) -> p j t", t=8)
                dve(lambda e, sg=sg, c=c: e.tensor_scalar(out=Y[c][:, 0:512], in0=sg[:, 2:2 + 512], scalar1=g1024[:, c, 0:1], scalar2=g1024[:, c, 31:32],
                                                          op0=ALU.mult, op1=ALU.add), [sg, g1024], [Y[c]])
                dve(lambda e, ss=ss, c=c, yv=yv: e.tensor_scalar(out=yv, in0=ss[:, :, 0:8], scalar1=g1024[:, c, 0:1], scalar2=g1024[:, c, 31:32],
                                                                 op0=ALU.mult, op1=ALU.add), [ss, g1024], [Y[c]])
                for k in range(1, 31):
                    dve(lambda e, sg=sg, c=c, k=k: e.scalar_tensor_tensor(out=Y[c][:, 0:512], in0=sg[:, 2 + k:2 + k + 512], scalar=g1024[:, c, k:k + 1],
                                                                         in1=Y[c][:, 0:512], op0=ALU.mult, op1=ALU.add), [sg, g1024, Y[c]], [Y[c]])
                    dve(lambda e, ss=ss, c=c, k=k, yv=yv: e.scalar_tensor_tensor(out=yv, in0=ss[:, :, k:k + 8], scalar=g1024[:, c, k:k + 1],
                                                                                in1=yv, op0=ALU.mult, op1=ALU.add), [ss, g1024, Y[c]], [Y[c]])
            if gidx == 1:
                transpose_out(hist_u.ap, [hist_u] * 8, 0, 30, o_cc_p[:, :], nchunks=8)
            a1 = next_acc(); a2 = next_acc()
            for c in range(8):
                def f1(e, c=c):
                    for a in range(nh):
                        last = e.matmul(a1[:, a, 0:hw], lhsT=ones_f[:, :], rhs=Y[c][:, a * hw:(a + 1) * hw], start=(c == 0), stop=(c == 7))
                    return last
                P.op("pe", f1, [ones_f, Y[c]], [a1])
                ta = tmpA[0]
                act(lambda e, c=c, ta=ta: e.activation(out=ta[:, 0:T], in_=Y[c][:, 0:T], func=AF.Square), [Y[c]], [ta])

                def f2(e, c=c, ta=ta):
                    for a in range(nh):
                        last = e.matmul(a2[:, a, 0:hw], lhsT=ones_f[:, :], rhs=ta[:, a * hw:(a + 1) * hw], start=(c == 0), stop=(c == 7))
                    return last
                P.op("pe", f2, [ones_f, ta], [a2])
            act(lambda e: e.activation(out=V(aux[:, 0:T], hw), in_=a1[:, 0:nh, 0:hw], func=AF.Copy, scale=1.0 / 1024), [a1], [aux])
            ta = tmpA[0]
            dve(lambda e: e.tensor_tensor(out=ta[:, 0:T], in0=aux[:, 0:T], in1=aux[:, 0:T], op=ALU.mult), [aux], [ta])
            dve(lambda e: e.scalar_tensor_tensor(out=V(rstd[:, 0:T], hw), in0=a2[:, 0:nh, 0:hw], scalar=1.0 / 1024, in1=V(ta[:, 0:T], hw),
                                                 op0=ALU.mult, op1=ALU.subtract), [a2, ta], [rstd])
            act(lambda e: e.activation(out=rstd[:, 0:T], in_=rstd[:, 0:T], func=AF.Ln, bias=EPS), [rstd], [rstd])
            act(lambda e: e.activation(out=rstd[:, 0:T], in_=rstd[:, 0:T], func=AF.Exp, scale=-0.5), [rstd], [rstd])
            for c in range(8):
                dve(lambda e, c=c: e.tensor_tensor(out=ta[:, 0:T], in0=Y[c][:, 0:T], in1=aux[:, 0:T], op=ALU.subtract), [Y[c], aux], [ta])
                dve(lambda e: e.tensor_tensor(out=ta[:, 0:T], in0=ta[:, 0:T], in1=rstd[:, 0:T], op=ALU.mult), [ta, rstd], [ta])
                act(lambda e, c=c: e.activation(out=H[16 + c][:, 0:T], in_=ta[:, 0:T], func=AF.Silu, scale=g1024[:, c, 32:33], bias=g1024[:, c, 33:34]),
                    [ta, g1024], [H[16 + c]])
            if dbg_stop == "cc":
                for c in range(8):
                    dve(lambda e, c=c: e.tensor_copy(out=Y[c][:, 0:T], in_=H[16 + c][:, 0:T]), [H[16 + c]], [Y[c]])
                for tt in range(4):
                    transpose_out(Y_t, Y, tt * 128, 128, y_main[tt * 128:(tt + 1) * 128, 0:1024], nchunks=8)
                transpose_out(Y_t, Y, 512, 64, y_sam[0:64, 0:1024], nchunks=8)
                return True
            fence(MG + ALIAS)
            for c in range(NCH):
                ws = wreq(w_sso, c, 128, 16); SA = mm(ws, 16, 128, H[0:16], T)
                wg_ = wreq(win_gate, c, 128, 16); GS = mm(wg_, 16, 128, xn, T)
                ta = tmpA[c % 2]
                act(lambda e, GS=GS, ta=ta: e.activation(out=V(ta[:, 0:T], hw), in_=GS[:, 0:nh, 0:hw], func=AF.Sigmoid), [GS], [ta])
                dve(lambda e, SA=SA, ta=ta: e.tensor_tensor(out=V(ta[:, 0:T], hw), in0=V(ta[:, 0:T], hw), in1=SA[:, 0:nh, 0:hw], op=ALU.mult), [SA, ta], [ta])
                wc = wreq(w_cco, c, 128, 8); CA = mm(wc, 8, 128, H[16:24], T)
                wg2 = wreq(win_gate, 16 + c, 128, 16); GC = mm(wg2, 16, 128, xn, T)
                act(lambda e, GC=GC: e.activation(out=V(rstd[:, 0:T], hw), in_=GC[:, 0:nh, 0:hw], func=AF.Sigmoid), [GC], [rstd])
                dve(lambda e, CA=CA: e.tensor_tensor(out=V(rstd[:, 0:T], hw), in0=V(rstd[:, 0:T], hw), in1=CA[:, 0:nh, 0:hw], op=ALU.mult), [CA, rstd], [rstd])
                dve(lambda e, c=c, ta=ta: e.tensor_tensor(out=MG[c][:, 0:T], in0=ta[:, 0:T], in1=rstd[:, 0:T], op=ALU.add), [ta, rstd], [MG[c]])
            for c in range(NCH):
                wo_ = wreq(w_o, c, 128, 16); A = mm(wo_, 16, 128, MG, T)
                act(lambda e, A=A, c=c: e.activation(out=V(Y[c][:, 0:T], hw), in_=A[:, 0:nh, 0:hw], func=AF.Copy), [A], [Y[c]])
            post_norm_residual(T, 3, 1.0)
            if dbg_stop == "mix":
                for tt in range(4):
                    transpose_out(R_t, R, tt * 128, 128, y_main[tt * 128:(tt + 1) * 128, :])
                transpose_out(R_t, R, 512, 64, y_sam[0:64, :])
                return True
            ffn(T, f2g, f2u, f2d, 4, 5)
            for tt in range(4):
                transpose_out(R_t, R, tt * 128, 128, y_main[gidx * 512 + tt * 128: gidx * 512 + (tt + 1) * 128, :])
            transpose_out(R_t, R, 512, 64, y_sam[gidx * 64:(gidx + 1) * 64, :])
        for gi_, (kind, gidx) in enumerate(groups):
            if group_body(gi_, kind, gidx):
                break
        P.emit()
    return nc


def _prep_inputs(inp):
    f = lambda a: np.ascontiguousarray(np.asarray(a, dtype=np.float32))
    xp = f(inp["x_prompt"]); xs = f(inp["x_sample"])
    vecD = np.stack([f(inp[k])[0] for k in ("ffn1_pre_norm", "ffn1_post_norm", "mix_pre_norm", "mix_post_norm",
                                            "ffn2_pre_norm", "ffn2_post_norm", "ssm_norm")])
    vec3072 = np.concatenate([f(inp["ssm_conv_w"])[0], f(inp["ssm_conv_b"])], axis=0)
    vec1024 = np.concatenate([f(inp["cc_conv_w"])[0], f(inp["cc_conv_b"]), f(inp["cc_ln_g"]), f(inp["cc_ln_b"])], axis=0)
    vecH = np.concatenate([f(inp["ssm_dt_bias"]), f(inp["ssm_A_log"]), f(inp["ssm_D"])], axis=0)
    def tile_w(W, KC, ncols=128):
        NC = W.shape[1] // ncols
        return np.ascontiguousarray(W.reshape(KC, 128, NC, ncols).transpose(2, 1, 0, 3))

    def tile_down(W):
        return np.concatenate([tile_w(W[h * 2816:(h + 1) * 2816], 22) for h in range(2)], axis=0)
    w_in_full = f(inp["w_in"])[0]
    shared = dict(vecD=vecD, vec3072=vec3072, vec1024=vec1024, vecH=vecH,
                  f1g=tile_w(f(inp["ffn1_w_gate"])[0], 16), f1u=tile_w(f(inp["ffn1_w_up"])[0], 16), f1d=tile_down(f(inp["ffn1_w_down"])[0]),
                  f2g=tile_w(f(inp["ffn2_w_gate"])[0], 16), f2u=tile_w(f(inp["ffn2_w_up"])[0], 16), f2d=tile_down(f(inp["ffn2_w_down"])[0]),
                  win_z=tile_w(w_in_full[:, 0:2048], 16), win_xbc=tile_w(w_in_full[:, 2048:5120], 16),
                  win_dt=tile_w(w_in_full[:, 5120:5152], 16, 32), win_glu=tile_w(w_in_full[:, 5152:7200], 16),
                  win_gate=tile_w(w_in_full[:, 7200:11296], 16),
                  w_sso=tile_w(f(inp["w_ssd_out"])[0], 16), w_cco=tile_w(f(inp["w_cc_out"])[0], 8), w_o=tile_w(f(inp["w_o"])[0], 16))
    maps = []
    for core in range(8):
        b, half = core // 2, core % 2
        m = dict(shared)
        m["xmain"] = xp[b, half * 1024:(half + 1) * 1024]
        m["xpre"] = xp[b, 0:1024] if half == 1 else np.zeros((1024, D), np.float32)
        m["flag"] = np.full((128, 1), float(half), np.float32)
        sl = slice(core * 16, (core + 1) * 16)
        m["xsam"] = xs[sl].reshape(128, D)
        m["st_ssm"] = f(inp["state_ssm"])[0, sl].reshape(16, 2048, 128)
        m["st_sconv"] = f(inp["state_ssm_conv"])[0, sl].reshape(48, 3072)
        m["st_cc"] = f(inp["state_cc_conv"])[0, sl].reshape(480, 1024)
        maps.append(m)
    return maps


def kernel(**inputs):
    nc = build()
    maps = _prep_inputs(inputs)
    res = run_bass_kernel_spmd(nc, maps, core_ids=list(range(8))).results
    yp = np.zeros((4, 2048, D), np.float32)
    for core in range(8):
        yp[core // 2, (core % 2) * 1024:(core % 2 + 1) * 1024] = res[core]["y_main"]
    ys = np.concatenate([res[c]["y_sam"].reshape(16, 8, D) for c in range(8)], axis=0)
    ssm_p = np.stack([res[2 * b + 1]["o_ssm_p"].reshape(32, 64, 128) for b in range(4)])[None]
    sconv_p = np.stack([res[2 * b + 1]["o_sconv_p"] for b in range(4)])[None]
    cc_p = np.stack([res[2 * b + 1]["o_cc_p"] for b in range(4)])[None]
    ssm_s = np.concatenate([res[c]["o_ssm_s"].reshape(16, 32, 64, 128) for c in range(8)], axis=0)[None]
    sconv_s = np.concatenate([res[c]["o_sconv_s"].reshape(16, 3, 3072) for c in range(8)], axis=0)[None]
    cc_s = np.concatenate([res[c]["o_cc_s"].reshape(16, 30, 1024) for c in range(8)], axis=0)[None]
    return (yp, ys, ssm_p, sconv_p, cc_p, ssm_s, sconv_s, cc_s)
```
